# Optimizing a Trainium2 kernel written in Bass

```python
import math
import jax, jax.numpy as jnp
from jax import lax
import numpy as np

D_MODEL = 2048
BATCH = 16
SEQ = 256
DEPTH = 2
DEC_BATCH = 2
DEC_SEQ = 1024
PAST_LEN = 256

GRID_W = 64
N_MIXERS = 2
N_A = (DEPTH + 1) // 2
N_B = DEPTH // 2
DN_DK = 128
DN_DV = 128
DN_HEADS_K = D_MODEL // DN_DK
DN_HEADS_V = 2 * DN_HEADS_K
DN_K_DIM = DN_HEADS_K * DN_DK
DN_V_DIM = DN_HEADS_V * DN_DV
DN_QKV_DIM = 2 * DN_K_DIM + DN_V_DIM
DN_PROJ = DN_QKV_DIM + DN_V_DIM + 4 * DN_HEADS_V
DN_CONV = 5
DN_CHUNK = 64
CM_DIM = 2 * D_MODEL
CM_CHUNK = 128
CM_GROUPS = 16
CM_GDIM = CM_DIM // CM_GROUPS
FF_DIM = 4 * D_MODEL
N_MOD = 6
EPS = 1e-6

kernel_name = 'hybrid_deltanet_chunkmlp_diffusion_step'


def rms_norm(x, w):
    xf = x.astype(jnp.float32)
    y = xf * lax.rsqrt(jnp.mean(xf * xf, axis=-1, keepdims=True) + EPS)
    return (y * w.astype(jnp.float32)).astype(x.dtype)


def layer_norm(x, w, b):
    xf = x.astype(jnp.float32)
    mu = jnp.mean(xf, axis=-1, keepdims=True)
    xc = xf - mu
    y = xc * lax.rsqrt(jnp.mean(xc * xc, axis=-1, keepdims=True) + EPS)
    return (y * w.astype(jnp.float32) + b.astype(jnp.float32)).astype(x.dtype)


def l2_norm(x):
    xf = x.astype(jnp.float32)
    return xf * lax.rsqrt(jnp.sum(xf * xf, axis=-1, keepdims=True) + EPS)


def grid_pos_embed(n_tokens, dtype):
    rows = n_tokens // GRID_W
    r = jnp.repeat(jnp.arange(rows), GRID_W).astype(jnp.float32)
    col = jnp.tile(jnp.arange(GRID_W), rows).astype(jnp.float32)
    quarter = D_MODEL // 4
    freq = 1.0 / (10000.0 ** (jnp.arange(quarter, dtype=jnp.float32) / quarter))
    ar = r[:, None] * freq[None, :]
    ac = col[:, None] * freq[None, :]
    pe = jnp.concatenate([jnp.sin(ar), jnp.cos(ar), jnp.sin(ac), jnp.cos(ac)], axis=-1)
    return pe.astype(dtype)


def adaln(cond, w_ada, b_ada):
    mod = (jax.nn.silu(cond) @ w_ada + b_ada)[:, None, :]
    return jnp.split(mod, N_MOD, axis=-1)


def modulate(h, shift, scale):
    return h * (1 + scale) + shift


def short_conv_centred(x, w):
    k_w = w.shape[0]
    pad = k_w // 2
    n = x.shape[1]
    xp = jnp.pad(x, ((0, 0), (pad, pad), (0, 0)))
    return sum(xp[:, j:j + n] * w[j] for j in range(k_w))


def gated_delta_chunked(q, k, v, log_g, beta, s0):
    b_sz, n_tok, n_h, dk = q.shape
    dv = v.shape[-1]
    c_len = DN_CHUNK
    n_ch = n_tok // c_len
    f32 = jnp.float32

    def chunks(t):
        t = t.astype(f32).reshape((b_sz, n_ch, c_len, n_h) + t.shape[3:])
        return jnp.moveaxis(t, (1, 3), (0, 2))

    qc, kc, vc = chunks(q), chunks(k), chunks(v)
    gc = jnp.cumsum(chunks(log_g), axis=-1)
    bc = chunks(beta)
    causal = jnp.tril(jnp.ones((c_len, c_len), dtype=bool))
    strict = jnp.tril(jnp.ones((c_len, c_len), dtype=bool), -1)
    decay = jnp.exp(jnp.where(causal, gc[..., :, None] - gc[..., None, :], -jnp.inf))
    kb = kc * bc[..., None]
    a_mat = jnp.where(strict, jnp.einsum('nbhik,nbhjk->nbhij', kb, kc) * decay, 0.0)
    t_sys = a_mat + jnp.eye(c_len, dtype=f32)
    rhs = jnp.concatenate([vc * bc[..., None], kb * jnp.exp(gc)[..., None]], axis=-1)
    sol = lax.linalg.triangular_solve(t_sys, rhs, left_side=True, lower=True, unit_diagonal=True)
    w_val, u_key = sol[..., :dv], sol[..., dv:]
    intra = jnp.einsum('nbhik,nbhjk->nbhij', qc, kc) * decay
    q_dec = qc * jnp.exp(gc)[..., None]
    g_last = gc[..., -1]
    k_dec = kc * jnp.exp(g_last[..., None] - gc)[..., None]

    def step(s, xs):
        w_i, u_i, intra_i, qd_i, kd_i, gl_i = xs
        v_new = w_i - jnp.einsum('bhck,bhkv->bhcv', u_i, s)
        o_i = jnp.einsum('bhck,bhkv->bhcv', qd_i, s) + jnp.einsum('bhij,bhjv->bhiv', intra_i, v_new)
        s = s * jnp.exp(gl_i)[..., None, None] + jnp.einsum('bhck,bhcv->bhkv', kd_i, v_new)
        return s, o_i

    s_fin, o = lax.scan(step, s0.astype(f32), (w_val, u_key, intra, q_dec, k_dec, g_last))
    o = jnp.moveaxis(o, (0, 2), (1, 3)).reshape(b_sz, n_tok, n_h, dv)
    return o, s_fin


def deltanet_mixer(h, s0_fwd, s0_bwd, w_in, conv_w, a_log, dt_bias, norm_w, w_out):
    b_sz, n_tok, _ = h.shape
    p = h @ w_in
    qkv = p[..., :DN_QKV_DIM]
    z = p[..., DN_QKV_DIM:DN_QKV_DIM + DN_V_DIM]
    ab = p[..., DN_QKV_DIM + DN_V_DIM:].reshape(b_sz, n_tok, 2, 2, DN_HEADS_V)
    qkv = jax.nn.silu(short_conv_centred(qkv, conv_w))
    q = qkv[..., :DN_K_DIM].reshape(b_sz, n_tok, DN_HEADS_K, DN_DK)
    k = qkv[..., DN_K_DIM:2 * DN_K_DIM].reshape(b_sz, n_tok, DN_HEADS_K, DN_DK)
    v = qkv[..., 2 * DN_K_DIM:].reshape(b_sz, n_tok, DN_HEADS_V, DN_DV)
    rep = DN_HEADS_V // DN_HEADS_K
    q = jnp.repeat(l2_norm(q) * (DN_DK ** -0.5), rep, axis=2)
    k = jnp.repeat(l2_norm(k), rep, axis=2)
    ab = ab.astype(jnp.float32)
    log_g = -jnp.exp(a_log.astype(jnp.float32)) * jax.nn.softplus(ab[..., 0, :] + dt_bias.astype(jnp.float32))
    beta = jax.nn.sigmoid(ab[..., 1, :])
    o_f, s_f = gated_delta_chunked(q, k, v, log_g[:, :, 0], beta[:, :, 0], s0_fwd)
    o_b, s_b = gated_delta_chunked(q[:, ::-1], k[:, ::-1], v[:, ::-1], log_g[:, ::-1, 1], beta[:, ::-1, 1], s0_bwd)
    o = o_f + o_b[:, ::-1]
    o = rms_norm(o, norm_w) * jax.nn.silu(z.reshape(b_sz, n_tok, DN_HEADS_V, DN_DV).astype(jnp.float32))
    out = o.reshape(b_sz, n_tok, DN_V_DIM).astype(h.dtype) @ w_out
    return out, s_f, s_b


def chunk_mlp_mixer(h, w_in, b_in, ln_w, ln_b, w_s, b_s, w_out):
    b_sz, n_tok, _ = h.shape
    zz = jax.nn.gelu(h @ w_in + b_in)
    u, v = jnp.split(zz, 2, axis=-1)
    v = layer_norm(v, ln_w, ln_b)
    v = v.reshape(b_sz, n_tok // CM_CHUNK, CM_CHUNK, CM_GROUPS, CM_GDIM)
    v = jnp.einsum('gpq,bnqgc->bnpgc', w_s, v) + b_s.T[None, None, :, :, None]
    return (u * v.reshape(b_sz, n_tok, CM_DIM)) @ w_out


def squared_relu_mlp(h, w1, w2):
    return jnp.square(jax.nn.relu(h @ w1)) @ w2


def setup_inputs(seed: int = 0) -> dict:
    key = jax.random.key(seed)
    ks = jax.random.split(key, 32)
    f32 = jnp.float32

    def nrm(k, shape, scale):
        return jax.random.normal(k, shape, f32) * scale

    dt = jnp.exp(jax.random.uniform(ks[13], (N_A, 2, DN_HEADS_V), f32, math.log(1e-3), math.log(1e-1)))
    return {
        'x_prompt': nrm(ks[0], (BATCH, SEQ, D_MODEL), 1.0),
        'x_sample': nrm(ks[1], (DEC_BATCH, DEC_SEQ, D_MODEL), 1.0),
        'state_dn_fwd': nrm(ks[2], (DEC_BATCH, N_A, DN_HEADS_V, DN_DK, DN_DV), 0.2),
        'state_dn_bwd': nrm(ks[3], (DEC_BATCH, N_A, DN_HEADS_V, DN_DK, DN_DV), 0.2),
        'c': nrm(ks[4], (DEC_BATCH, D_MODEL), 1.0),
        'c_ctx': nrm(ks[5], (D_MODEL,), 1.0),
        'norm_mix_w': 1.0 + nrm(ks[6], (DEPTH, D_MODEL), 0.01),
        'norm_mlp_w': 1.0 + nrm(ks[7], (DEPTH, D_MODEL), 0.01),
        'w_ada': nrm(ks[8], (DEPTH, D_MODEL, N_MOD * D_MODEL), 0.5 * D_MODEL ** -0.5),
        'b_ada': nrm(ks[9], (DEPTH, N_MOD * D_MODEL), 0.01),
        'dn_w_in': nrm(ks[10], (N_A, D_MODEL, DN_PROJ), D_MODEL ** -0.5),
        'dn_conv_w': nrm(ks[11], (N_A, DN_CONV, DN_QKV_DIM), DN_CONV ** -0.5),
        'dn_A_log': jnp.log(jax.random.uniform(ks[12], (N_A, 2, DN_HEADS_V), f32, 1.0, 16.0)),
        'dn_dt_bias': dt + jnp.log(-jnp.expm1(-dt)),
        'dn_norm_w': 1.0 + nrm(ks[14], (N_A, DN_DV), 0.01),
        'dn_w_out': nrm(ks[15], (N_A, DN_V_DIM, D_MODEL), DN_V_DIM ** -0.5),
        'cm_w_in': nrm(ks[16], (N_B, D_MODEL, 2 * CM_DIM), D_MODEL ** -0.5),
        'cm_b_in': nrm(ks[17], (N_B, 2 * CM_DIM), 0.01),
        'cm_ln_w': 1.0 + nrm(ks[18], (N_B, CM_DIM), 0.01),
        'cm_ln_b': nrm(ks[19], (N_B, CM_DIM), 0.01),
        'cm_w_s': nrm(ks[20], (N_B, CM_GROUPS, CM_CHUNK, CM_CHUNK), CM_CHUNK ** -0.5),
        'cm_b_s': 1.0 + nrm(ks[21], (N_B, CM_GROUPS, CM_CHUNK), 0.01),
        'cm_w_out': nrm(ks[22], (N_B, CM_DIM, D_MODEL), CM_DIM ** -0.5),
        'w_ff1': nrm(ks[23], (DEPTH, D_MODEL, FF_DIM), D_MODEL ** -0.5),
        'w_ff2': nrm(ks[24], (DEPTH, FF_DIM, D_MODEL), FF_DIM ** -0.5),
        'final_norm_w': 1.0 + nrm(ks[25], (D_MODEL,), 0.01),
    }


def reference(x_prompt, x_sample, state_dn_fwd, state_dn_bwd, c, c_ctx, norm_mix_w, norm_mlp_w,
              w_ada, b_ada, dn_w_in, dn_conv_w, dn_A_log, dn_dt_bias, dn_norm_w, dn_w_out,
              cm_w_in, cm_b_in, cm_ln_w, cm_ln_b, cm_w_s, cm_b_s, cm_w_out, w_ff1, w_ff2, final_norm_w):
    ctx = x_prompt
    lat = x_sample + grid_pos_embed(x_sample.shape[1], x_sample.dtype)[None]
    cond_ctx = c_ctx[None, :]
    zero_state = jnp.zeros((x_prompt.shape[0], DN_HEADS_V, DN_DK, DN_DV), jnp.float32)
    new_fwd, new_bwd = [], []
    for i in range(DEPTH):
        j = i // N_MIXERS
        sh_c, sc_c, g_c, sh2_c, sc2_c, g2_c = adaln(cond_ctx, w_ada[i], b_ada[i])
        sh_l, sc_l, g_l, sh2_l, sc2_l, g2_l = adaln(c, w_ada[i], b_ada[i])
        h_c = modulate(rms_norm(ctx, norm_mix_w[i]), sh_c, sc_c)
        h_l = modulate(rms_norm(lat, norm_mix_w[i]), sh_l, sc_l)
        if i % N_MIXERS == 0:
            dn = (dn_w_in[j], dn_conv_w[j], dn_A_log[j], dn_dt_bias[j], dn_norm_w[j], dn_w_out[j])
            m_c, s_f, s_b = deltanet_mixer(h_c, zero_state, zero_state, *dn)
            m_l, _, _ = deltanet_mixer(h_l, state_dn_fwd[:, j], state_dn_bwd[:, j], *dn)
            new_fwd.append(s_f.astype(x_prompt.dtype))
            new_bwd.append(s_b.astype(x_prompt.dtype))
        else:
            cm = (cm_w_in[j], cm_b_in[j], cm_ln_w[j], cm_ln_b[j], cm_w_s[j], cm_b_s[j], cm_w_out[j])
            m_c = chunk_mlp_mixer(h_c, *cm)
            m_l = chunk_mlp_mixer(h_l, *cm)
        ctx = ctx + g_c * m_c
        lat = lat + g_l * m_l
        ctx = ctx + g2_c * squared_relu_mlp(modulate(rms_norm(ctx, norm_mlp_w[i]), sh2_c, sc2_c), w_ff1[i], w_ff2[i])
        lat = lat + g2_l * squared_relu_mlp(modulate(rms_norm(lat, norm_mlp_w[i]), sh2_l, sc2_l), w_ff1[i], w_ff2[i])
    y_prompt = rms_norm(ctx, final_norm_w)
    y_sample = rms_norm(lat, final_norm_w)
    new_state_dn_fwd = jnp.stack(new_fwd, axis=1)
    new_state_dn_bwd = jnp.stack(new_bwd, axis=1)
    return (y_prompt, y_sample, new_state_dn_fwd, new_state_dn_bwd)
```

```python
import math
import os
from contextlib import ExitStack
import numpy as np
import concourse.bass as bass
import concourse.mybir as mybir
from concourse.bass_utils import run_bass_kernel_spmd

F32 = mybir.dt.float32
BF16 = mybir.dt.bfloat16
AF = mybir.ActivationFunctionType
ALU = mybir.AluOpType
ENGS = ("pe", "act", "dve", "pool", "sp")

D = 2048
T = 1024
NT = 8
NG = int(os.environ.get('KNG', '16'))
KC = 16
EPS = 1e-6
import os
DBG = bool(os.environ.get('KDBG'))
STAGE = int(os.environ.get('KSTAGE', '9'))
KDN = int(os.environ.get('KDN', '9'))


class Buf:
    __slots__ = ("name", "lw", "rd", "excl")

    def __init__(self, name, excl=False):
        self.name = name
        self.lw = None
        self.rd = []
        self.excl = excl


class Sched:
    def __init__(self, nc):
        self.nc = nc
        self.ops = {e: [] for e in ENGS}
        self.cnt = {e: 0 for e in ENGS}
        self.dma_sems = []
        self.waited = {e: {} for e in ENGS}
        self.final_dma = []
        self.live_dma = []
        self.pool = {}
        self.pool_n = {}
        self.pool_last = {}

    def _deps(self, reads, writes):
        deps = []
        for b in reads:
            if b.lw is not None:
                deps.append(b.lw)
            if b.excl:
                deps.extend(b.rd)
        for b in writes:
            if b.lw is not None:
                deps.append(b.lw)
            deps.extend(b.rd)
        return deps

    def _resolve(self, eng, deps):
        w = {}
        for d in deps:
            if d[0] == "eng":
                _, e, c = d
                if e == eng and eng in ("pe", "sp"):
                    continue
                key, val = ("eng", e), c
            else:
                key, val = ("dma", d[1]), d[2]
            if self.waited[eng].get(key, 0) >= val:
                continue
            if w.get(key, 0) < val:
                w[key] = val
        for k, v in w.items():
            self.waited[eng][k] = v
        return list(w.items())

    def _mark(self, tag, reads, writes):
        for b in reads:
            if b.excl:
                b.rd = [t for t in b.rd if t[1] != tag[1] or t[0] != tag[0]] + [tag]
            else:
                b.rd.append(tag)
        for b in writes:
            b.lw = tag
            b.rd = []

    limit = None

    def _budget(self):
        if self.limit is None:
            return True
        if self.limit <= 0:
            return False
        self.limit -= 1
        return True

    def op(self, eng, fn, reads=(), writes=()):
        if not self._budget():
            return None
        waits = self._resolve(eng, self._deps(reads, writes))
        self.cnt[eng] += 1
        tag = ("eng", eng, self.cnt[eng])
        self.ops[eng].append(("op", fn, waits))
        self._mark(tag, reads, writes)
        return tag

    def new_dma_sem(self):
        self.dma_sems.append(0)
        return len(self.dma_sems) - 1

    def dma(self, queue, out, in_, reads=(), writes=(), sem=None, final=False):
        if not self._budget():
            return None
        extra = []
        if sem is None:
            K = 12
            pl = self.pool.setdefault(queue, [])
            n = self.pool_n.get(queue, 0)
            self.pool_n[queue] = n + 1
            if len(pl) < K:
                pl.append(self.new_dma_sem())
            sem = pl[n % K]
            if sem in self.pool_last:
                extra = [self.pool_last[sem]]
        waits = self._resolve(queue, self._deps(reads, writes) + extra)
        self.dma_sems[sem] += 16
        tag = ("dma", sem, self.dma_sems[sem])
        self.ops[queue].append(("dma", (out, in_, sem), waits))
        self._mark(tag, reads, writes)
        self.pool_last[sem] = tag
        self.live_dma.append(tag)
        if final:
            self.final_dma.append(tag)
        return tag

    def fence(self):
        deps = [("eng", e, self.cnt[e]) for e in ENGS if e != "sp" and self.cnt[e] > 0]
        deps += self.live_dma
        self.live_dma = []
        waits = self._resolve("sp", deps)
        self.cnt["sp"] += 1
        tag = ("eng", "sp", self.cnt["sp"])
        self.ops["sp"].append(("op", lambda e: e.nop(), waits))
        for e in ENGS:
            if e == "sp":
                continue
            w = self._resolve(e, [tag])
            if w:
                self.ops[e].append(("wait", None, w))

    def emit(self, stack):
        nc = self.nc
        esem = {e: stack.enter_context(nc.semaphore("s_" + e)) for e in ENGS}
        dsem = [stack.enter_context(nc.semaphore("d_%d" % i)) for i in range(len(self.dma_sems))]
        block = stack.enter_context(nc.Block())

        def semof(key):
            return esem[key[1]] if key[0] == "eng" else dsem[key[1]]

        def run(engname, engine):
            for kind, payload, waits in self.ops[engname]:
                for key, val in waits:
                    engine.wait_ge(semof(key), val)
                if kind == "op":
                    payload(engine).then_inc(esem[engname], 1)
                elif kind == "dma":
                    out, in_, si = payload
                    engine.dma_start(out=out, in_=in_).then_inc(dsem[si], 16)
            if engname == "sp":
                for tag in self.final_dma:
                    engine.wait_ge(dsem[tag[1]], tag[2])

        block.sync(lambda e: run("sp", e))
        block.tensor(lambda e: run("pe", e))
        block.scalar(lambda e: run("act", e))
        block.vector(lambda e: run("dve", e))
        block.gpsimd(lambda e: run("pool", e))


def build():
    nc = bass.Bass("TRN2", target_bir_lowering=False)
    S = Sched(nc)

    def din(name, shape):
        return nc.dram_tensor("i_" + name, list(shape), F32, kind="ExternalInput").ap()

    def dout(name, shape):
        return nc.dram_tensor("o_" + name, list(shape), F32, kind="ExternalOutput").ap()

    x_d = din("x", [T, D])
    flag_d = din("flag", [128, 1])
    cond_d = din("cond", [128, KC])
    s0_d = [din("s0f", [32, 128, 128]), din("s0b", [32, 128, 128])]
    cst_d = din("cst", [128, 15, 128])
    posc_d = din("posc", [128, 4 + 16 + 64])
    nmw_d = din("nmw", [2, 128, KC])
    nlw_d = din("nlw", [2, 128, KC])
    fnw_d = din("fnw", [128, KC])
    bada_d = din("bada", [2, 128, 96])
    wada_d = din("w_ada", [2, D, 6 * D])
    dnwin_d = din("dn_w_in", [D, 12416])
    convw_d = din("convw", [128, 64, 5])
    alog_d = din("alog", [128, 64])
    dtb_d = din("dtb", [128, 64])
    dnnw_d = din("dnnw", [128, 1])
    dnwout_d = din("dn_w_out", [4096, D])
    cmwin_d = din("cm_w_in", [D, 8192])
    cmbin_d = din("cmbin", [128, 64])
    cmlnw_d = din("cmlnw", [128, 32])
    cmlnb_d = din("cmlnb", [128, 32])
    cmws_d = din("cm_w_s", [16, 128, 128])
    cmbs_d = din("cmbs", [1, 16 * 128])
    cmwout_d = din("cm_w_out", [4096, D])
    wff1_d = din("w_ff1", [2, D, 8192])
    wff2_d = din("w_ff2", [2, 8192, D])
    y_d = dout("y", [T, D])
    ns_d = [dout("nsf", [4, 32, 128, 128]), dout("nsb", [4, 32, 128, 128])]
    dbg_d = dout("dbg", [8, 128, KC, T]) if DBG else None

    sb = nc.alloc_sbuf_tensor
    xT = sb("xT", [128, KC, T], F32)
    hT = sb("hT", [128, KC, T], BF16)
    NSLOT = 4
    wsl = [sb("wsl%d" % i, [128, 16, 128], BF16) for i in range(NSLOT)]
    cst = sb("cst", [128, 15, 128], F32)
    flag = sb("flag", [128, 1], F32)
    cond = sb("cond", [128, KC], F32)
    scond = sb("scond", [128, KC, 128], BF16)
    mod = [sb("mod%d" % l, [128, 96], F32) for l in range(2)]
    bada = [sb("bada%d" % l, [128, 96], F32) for l in range(2)]
    nmw = sb("nmw", [128, 2, KC], F32)
    nlw = sb("nlw", [128, 2, KC], F32)
    fnw = sb("fnw", [128, KC], F32)
    geff = sb("geff", [128, KC], F32)
    rstd = sb("rstd", [128, T], F32)
    tmpa = [sb("tmpa%d" % i, [128, T], F32) for i in range(2)]

    RW = (nc.sbuf_bytes_remaining - 2560) // 4 // 8 * 8
    print("region cols", RW)
    Rg = sb("Rg", [128, RW], F32)
    rst = {"off": 0}

    def rreset():
        rst["off"] = 0

    def ralloc(shape, dt=F32):
        n = 1
        for v in shape[1:]:
            n *= v
        cols = n if dt == F32 else (n + 1) // 2
        cols = (cols + 7) // 8 * 8
        o = rst["off"]
        rst["off"] += cols
        rst["max"] = max(rst.get("max", 0), rst["off"])
        assert rst["off"] <= RW, ("region overflow", rst["off"])
        a = Rg[0:shape[0], o:o + cols]
        if dt != F32:
            a = a.bitcast(dt)
        a = a[:, 0:n]
        if len(shape) == 3:
            a = a.rearrange("p (a b) -> p a b", a=shape[1])
        elif len(shape) == 4:
            a = a.rearrange("p (a b c) -> p a b c", a=shape[1], b=shape[2])
        return a

    B = {}

    def buf(name):
        if name not in B:
            B[name] = Buf(name)
        return B[name]

    BxT = [buf("xT%d" % c) for c in range(KC)]
    BhT = [buf("hT%d" % c) for c in range(KC)]
    Bw = [buf("wsl%d" % i) for i in range(NSLOT)]
    wsem = [S.new_dma_sem() for _ in range(NSLOT)]
    Bcst, Bflag, Bcond, Bscond = buf("cst"), buf("flag"), buf("cond"), buf("scond")
    Bmod = [buf("mod0"), buf("mod1")]
    Bbada = [buf("bada0"), buf("bada1")]
    Bsm = buf("small")
    Bgeff, Brstd = buf("geff"), buf("rstd")
    Btmpa = [buf("tmpa0"), buf("tmpa1")]

    pbank = [nc.alloc_psum_tensor("pb%d" % i, [128, 512], F32) for i in range(8)]
    Bpb = [Buf("pb%d" % i, excl=True) for i in range(8)]
    st = {"big": 0, "sm": 0, "w": 0, "ta": 0}

    def big_bank():
        st["big"] ^= 1
        return st["big"]

    def sm_bank():
        st["sm"] = (st["sm"] + 1) % 6
        return 2 + st["sm"]

    IDENT, ONES, UF, UB = 0, 1, 2, 3
    BD16, ML0 = 4, 9

    def C(i):
        return cst[:, i, :]

    def mmg(bank, out_ap, terms):
        n = len(terms)
        for i, (l, r, rd) in enumerate(terms):
            S.op("pe", (lambda e, o=out_ap, l=l, r=r, i=i: e.matmul(o, l, r, start=(i == 0), stop=(i == n - 1))),
                 reads=list(rd), writes=[Bpb[bank]] if i == 0 else [])
        Bpb[bank].lw = ("eng", "pe", S.cnt["pe"])

    def mm1(bank, out_ap, l, r, rd):
        S.op("pe", lambda e: e.matmul(out_ap, l, r, start=True, stop=True), reads=list(rd), writes=[Bpb[bank]])

    def tr(bank, out_ap, in_ap, rd):
        S.op("pe", lambda e: e.transpose(out_ap, in_ap, C(IDENT)), reads=list(rd) + [Bcst], writes=[Bpb[bank]])

    def act(out, in_, func, rd, wr, bias=None, scale=None, eng="act"):
        kw = {}
        if bias is not None:
            kw["bias"] = bias
        if scale is not None:
            kw["scale"] = scale
        S.op("act", lambda e: e.activation(out=out, in_=in_, func=func, **kw), reads=rd, writes=wr)

    def tsc(eng, out, in0, s1, op0, rd, wr, s2=None, op1=None):
        if op1 is None:
            S.op(eng, lambda e: e.tensor_scalar(out=out, in0=in0, scalar1=s1, scalar2=None, op0=op0), reads=rd, writes=wr)
        else:
            S.op(eng, lambda e: e.tensor_scalar(out=out, in0=in0, scalar1=s1, scalar2=s2, op0=op0, op1=op1), reads=rd, writes=wr)

    def stt(out, in0, sc, in1, op0, op1, rd, wr):
        S.op("dve", lambda e: e.scalar_tensor_tensor(out=out, in0=in0, scalar=sc, in1=in1, op0=op0, op1=op1), reads=rd, writes=wr)

    def tt(eng, out, in0, in1, op, rd, wr):
        S.op(eng, lambda e: e.tensor_tensor(out=out, in0=in0, in1=in1, op=op), reads=rd, writes=wr)

    def cp(eng, out, in_, rd, wr):
        if eng == "act":
            S.op("act", lambda e: e.copy(out=out, in_=in_), reads=rd, writes=wr)
        else:
            S.op(eng, lambda e: e.tensor_copy(out=out, in_=in_), reads=rd, writes=wr)

    S.dma("sp", cst[:], cst_d, writes=[Bcst])
    S.dma("sp", flag[:], flag_d, writes=[Bflag])
    S.dma("sp", cond[:], cond_d, writes=[Bcond])
    for l in range(2):
        S.dma("sp", bada[l][:], bada_d[l], writes=[Bbada[l]])
        S.dma("sp", nmw[:, l, :], nmw_d[l], writes=[Bsm])
        S.dma("sp", nlw[:, l, :], nlw_d[l], writes=[Bsm])
    S.dma("sp", fnw[:], fnw_d, writes=[Bsm])

    def proj(w_ap, k0, kc_n, col_chunks, actT, act_bufs, tcols, evac, pre=None):
        ntb = (tcols + 511) // 512
        n_ = len(col_chunks)
        PF = NSLOT - 1
        slots = {}

        def issue(ci_):
            sl_ = st["w"] % NSLOT
            st["w"] += 1
            slots[ci_] = sl_
            c_ = col_chunks[ci_]
            src = w_ap[k0 * 128:(k0 + kc_n) * 128, c_ * 128:(c_ + 1) * 128].rearrange("(kc p) j -> p kc j", p=128)
            S.dma("pool", wsl[sl_][:, 0:kc_n, :], src, writes=[Bw[sl_]], sem=wsem[sl_])
        if pre:
            slots.update(pre)
        for ci_ in range(min(PF, n_)):
            if ci_ not in slots:
                issue(ci_)
        for ci, c in enumerate(col_chunks):
            if ci + PF < n_ and (ci + PF) not in slots:
                issue(ci + PF)
            sl = slots[ci]
            for tb in range(ntb):
                t0, t1 = tb * 512, min(tcols, tb * 512 + 512)
                bk = big_bank()
                terms = [(wsl[sl][:, kc, :], actT[:, kc, t0:t1], [Bw[sl], act_bufs[kc]]) for kc in range(kc_n)]
                mmg(bk, pbank[bk][:, 0:t1 - t0], terms)
                evac(ci, c, tb, bk, pbank[bk][:, 0:t1 - t0])

    def prefetch(w_ap, k0, kc_n, c_):
        sl_ = st["w"] % NSLOT
        st["w"] += 1
        src = w_ap[k0 * 128:(k0 + kc_n) * 128, c_ * 128:(c_ + 1) * 128].rearrange("(kc p) j -> p kc j", p=128)
        S.dma("pool", wsl[sl_][:, 0:kc_n, :], src, writes=[Bw[sl_]], sem=wsem[sl_])
        return sl_

    rreset()
    xin = [ralloc([128, D]) for i in range(2)]
    Bxin = [buf("xin0"), buf("xin1")]
    for tl in range(NT):
        xb = tl % 2
        S.dma("sp", xin[xb][:], x_d[tl * 128:(tl + 1) * 128, :], writes=[Bxin[xb]])
        for c4 in range(4):
            bk = sm_bank()
            for j in range(4):
                c = c4 * 4 + j
                S.op("pe", (lambda e, o=pbank[bk][:, j * 128:(j + 1) * 128], i=xin[xb][:, c * 128:(c + 1) * 128]:
                            e.transpose(o, i, C(IDENT))), reads=[Bxin[xb], Bcst], writes=[Bpb[bk]] if j == 0 else [])
            Bpb[bk].lw = ("eng", "pe", S.cnt["pe"])
            eng = "act" if c4 % 2 == 0 else "dve"
            cp(eng, xT[:, c4 * 4:(c4 + 1) * 4, tl * 128:(tl + 1) * 128],
               pbank[bk][:, :].rearrange("p (j t) -> p j t", j=4), [Bpb[bk]], [BxT[c4 * 4 + j] for j in range(4)])

    posc = sb("posc", [128, 84], F32)
    freq = sb("freq", [128, 4], F32)
    ang = sb("ang", [128, 64], F32)
    acc = sb("acc", [128, 64], F32)
    ptab = sb("ptab", [128, 64], F32)
    Bpos = buf("pos")
    S.dma("sp", posc[:], posc_d, writes=[Bpos])
    act(freq[:], posc[:, 0:4], AF.Exp, [Bpos], [Bpos], scale=-math.log(10000.0) / 512.0)
    TWO_PI = 2.0 * math.pi
    for c in range(KC):
        quarter, cq = c // 4, c % 4
        n = 16 if quarter < 2 else 64
        pv = posc[:, 4:20] if quarter < 2 else posc[:, 20:84]
        phase = 0.0 if quarter % 2 == 0 else math.pi / 2
        tsc("dve", ang[:, 0:n], pv, freq[:, cq:cq + 1], ALU.mult, [Bpos], [Bpos], s2=phase, op1=ALU.add)
        S.op("dve", lambda e, n=n: e.memset(acc[:, 0:n], 0.0), reads=[], writes=[Bpos])
        for m in range(1, 11):
            stt(acc[:, 0:n], ang[:, 0:n], TWO_PI * m, acc[:, 0:n], ALU.is_ge, ALU.add, [Bpos], [Bpos])
        stt(ang[:, 0:n], acc[:, 0:n], -TWO_PI, ang[:, 0:n], ALU.mult, ALU.add, [Bpos], [Bpos])
        tsc("dve", ang[:, 0:n], ang[:, 0:n], -1.0, ALU.mult, [Bpos], [Bpos], s2=math.pi, op1=ALU.add)
        act(ptab[:, 0:n], ang[:, 0:n], AF.Sin, [Bpos], [Bpos])
        tsc("dve", ptab[:, 0:n], ptab[:, 0:n], flag[:, 0:1], ALU.mult, [Bpos, Bflag], [Bpos])
        for r_ in range(16):
            xv = xT[:, c, r_ * 64:(r_ + 1) * 64]
            if quarter < 2:
                tsc("pool", xv, xv, ptab[:, r_:r_ + 1], ALU.add, [Bpos, BxT[c]], [BxT[c]])
            else:
                tt("pool", xv, xv, ptab[:, 0:64], ALU.add, [Bpos, BxT[c]], [BxT[c]])

    dbg_i = [0]

    def dump():
        if DBG:
            S.dma("sp", dbg_d[dbg_i[0]], xT[:], reads=BxT, final=True)
            dbg_i[0] += 1

    dump()

    tmpc = sb("tmpc", [128, KC], F32)
    act(tmpc[:], cond[:], AF.Silu, [Bcond], [Bsm])
    for kc_ in range(KC):
        tsc("dve", scond[:, kc_, :], C(ONES), tmpc[:, kc_:kc_ + 1], ALU.mult, [Bsm, Bcst], [Bscond])

    def adaln(l):
        def ev(ci, c, tb, bk, ps):
            tt("dve", mod[l][:, c:c + 1], ps[:, 0:1], bada[l][:, c:c + 1], ALU.add, [Bpb[bk], Bbada[l]], [Bmod[l]])
        proj(wada_d[l], 0, KC, list(range(96)), scond, [Bscond] * KC, 128, ev)

    def rms_stats():
        bks = [sm_bank(), sm_bank()]
        for c in range(KC):
            tb_ = c % 2
            act(tmpa[tb_][:], xT[:, c, :], AF.Square, [BxT[c]], [Btmpa[tb_]])
            for h2 in range(2):
                S.op("pe", (lambda e, o=pbank[bks[h2]][:, :], r=tmpa[tb_][:, h2 * 512:(h2 + 1) * 512], c=c:
                            e.matmul(o, C(ONES), r, start=(c == 0), stop=(c == KC - 1))),
                     reads=[Btmpa[tb_], Bcst], writes=[Bpb[bks[h2]]] if c == 0 else [])
        for h2 in range(2):
            Bpb[bks[h2]].lw = ("eng", "pe", S.cnt["pe"])
        for h2 in range(2):
            act(rstd[:, h2 * 512:(h2 + 1) * 512], pbank[bks[h2]][:, :], AF.Sqrt, [Bpb[bks[h2]]], [Brstd],
                bias=epsc[:, 0:1], scale=1.0 / D)
        S.op("dve", lambda e: e.reciprocal(out=rstd[:], in_=rstd[:]), reads=[Brstd], writes=[Brstd])

    epsc = sb("epsc", [128, 1], F32)
    S.op("dve", lambda e: e.memset(epsc[:], EPS), reads=[], writes=[Bsm])

    def norm_mod(l, wtile, shift_i, scale_i):
        rms_stats()
        stt(geff[:], mod[l][:, scale_i * 16:(scale_i + 1) * 16], 1.0, wtile, ALU.add, ALU.mult, [Bmod[l], Bsm], [Bgeff])
        for c in range(KC):
            tb_ = c % 2
            stt(tmpa[tb_][:], xT[:, c, :], geff[:, c:c + 1], rstd[:], ALU.mult, ALU.mult, [BxT[c], Bgeff, Brstd], [Btmpa[tb_]])
            act(hT[:, c, :], tmpa[tb_][:], AF.Identity, [Btmpa[tb_], Bmod[l]], [BhT[c]],
                bias=mod[l][:, shift_i * 16 + c:shift_i * 16 + c + 1])

    def ffn(l, uT, BuT):
        norm_mod(l, nlw[:, l, :], 3, 4)
        for J in range(4):
            def evA(ci, c, tb, bk, ps):
                e2 = tmpa[0][:, 0:512] if tb == 0 else tmpa[1][:, 0:512]
                be = Btmpa[0] if tb == 0 else Btmpa[1]
                act(e2, ps, AF.Relu, [Bpb[bk]], [be])
                act(uT[:, ci, tb * 512:(tb + 1) * 512], e2, AF.Square, [be], [BuT[ci]])
            proj(wff1_d[l], 0, KC, list(range(J * 16, J * 16 + 16)), hT, BhT, T, evA)

            def evB(ci, c, tb, bk, ps):
                xs = xT[:, c, tb * 512:(tb + 1) * 512]
                stt(xs, ps, mod[l][:, 5 * 16 + c:5 * 16 + c + 1], xs, ALU.mult, ALU.add, [Bpb[bk], Bmod[l], BxT[c]], [BxT[c]])
            proj(wff2_d[l], J * 16, 16, list(range(KC)), uT, BuT, T, evB)

    if STAGE >= 1:
        if not os.environ.get("KSKIP_ADA"):
            adaln(0)
        if not os.environ.get("KSKIP_NORM"):
            norm_mod(0, nmw[:, 0, :], 0, 1)
        if os.environ.get("KDBG_H"):
            for c in range(KC):
                cp("dve", xT[:, c, :], hT[:, c, :], [BhT[c]], [BxT[c]])
            dump()

    S.fence()
    rreset()
    if STAGE >= 2:
        def rs(name, shape, dt=F32):
            return ralloc(shape, dt)
        convw = rs("convw", [128, 64, 5])
        alog = rs("alog", [128, 64])
        dtb = rs("dtb", [128, 64])
        dnnw = rs("dnnw", [128, 1])
        Bdn = buf("dnsmall")
        S.dma("sp", convw[:], convw_d, writes=[Bdn])
        S.dma("sp", alog[:], alog_d, writes=[Bdn])
        S.dma("sp", dtb[:], dtb_d, writes=[Bdn])
        S.dma("sp", dnnw[:], dnnw_d, writes=[Bdn])
        sc_off = rst["off"]
        XS = []
        ALIAS = {"lgbc": 0, "A16": 0, "nuT": 0, "KK": 1, "Pb2": 1, "kdec": 1, "QK": 2, "inT": 2, "DT": 3, "vb": 3,
                 "Dm": 4, "Bm": 4, "kbg": 4, "EG": 5, "qd": 5, "A": 6, "Pa": 7, "Lb": 7, "vnew": 7, "Pb": 8, "Mt": 8,
                 "Pa2": 9, "V": 9, "Y": 10}
        for si_ in range(4):
            phys = [rs("sc%d_%d" % (si_, k_), [128, 128]) for k_ in range(11)]
            pb_ = [buf("sc%d_%d" % (si_, k_)) for k_ in range(11)]
            X = {}
            for nm, k_ in ALIAS.items():
                X[nm] = phys[k_]
                X["B" + nm] = pb_[k_]
            XS.append(X)
        gcg = rs("gcg", [128, NT, 4])
        ngcg = rs("ngcg", [128, NT, 4])
        eglg = rs("eglg", [128, NT, 4])
        edlg = rs("edlg", [128, NT, 4])
        kbsg = rs("kbsg", [128, NT, 4])
        Bgd = buf("gdecay")
        SS = [[rs("Sst%d_%d" % (d_, i), [128, 128]) for i in range(2)] for d_ in range(2)]
        BSS = [[buf("Sst%d_%d" % (d_, i)) for i in range(2)] for d_ in range(2)]
        wab = Rg[:, sc_off:sc_off + 1024].bitcast(BF16).rearrange("p (a b) -> p a b", a=KC)
        Bwab = buf("wab")
        S.dma("pool", wab[:], dnwin_d[:, 12288:12416].rearrange("(kc p) j -> p kc j", p=128), writes=[Bwab])
        lg = rs("lg", [128, NT, 64])
        beta = rs("beta", [128, NT, 64])
        Bab = buf("ab")
        S.op("act", lambda e: e.activation(out=alog[:], in_=alog[:], func=AF.Exp), reads=[Bdn], writes=[Bdn])
        for tl in range(NT):
            bk = sm_bank()
            mmg(bk, pbank[bk][:, 0:128], [(hT[:, kc, tl * 128:(tl + 1) * 128], wab[:, kc, :], [BhT[kc], Bwab]) for kc in range(KC)])
            for d in range(2):
                pa = pbank[bk][:, d * 64:d * 64 + 32]
                pb_ = pbank[bk][:, d * 64 + 32:d * 64 + 64]
                lgs = lg[:, tl, d * 32:(d + 1) * 32]
                tt("dve", lgs, pa, dtb[:, d * 32:(d + 1) * 32], ALU.add, [Bpb[bk], Bdn], [Bab])
                act(lgs, lgs, AF.Exp, [Bab], [Bab])
                act(lgs, lgs, AF.Ln, [Bab], [Bab], bias=1.0)
                stt(lgs, lgs, -1.0, alog[:, d * 32:(d + 1) * 32], ALU.mult, ALU.mult, [Bab, Bdn], [Bab])
                act(beta[:, tl, d * 32:(d + 1) * 32], pb_, AF.Sigmoid, [Bpb[bk]], [Bab])
        S.fence()
        pad = [rs("pad0", [128, 4, 260])]
        Bpad = [buf("pad0")]
        cvo = [rs("cvo0", [128, T])]
        Bcvo = [buf("cvo0")]
        qT = rs("qT", [128, T])
        kT = rs("kT", [128, T])
        zg = rs("zg", [128, T])
        ktok = rs("ktok", [128, NT, 128])
        vtok = rs("vtok", [128, NT, 128])
        oTf = rs("oTf", [128, T])
        og = rs("og", [128, 2, T], BF16)
        Bq, Bk, Bz, Bktok, Bvtok = buf("qT"), buf("kT"), buf("zg"), buf("ktok"), buf("vtok")
        BoTf, Bog = buf("oTf"), [buf("og0"), buf("og1")]
        S.op("pool", lambda e: e.memset(pad[0][:], 0.0), reads=[], writes=Bpad)

        def conv_chunk(chunk_idx, dst, Bdst, pre=None):
            pd = pad[0]

            def ev(ci, c, tb, bk, ps):
                cp("act", pd[:, tb * 2:tb * 2 + 2, 2:258], ps.rearrange("p (s t) -> p s t", s=2), [Bpb[bk]], [Bpad[0]])
            proj(dnwin_d, 0, KC, [chunk_idx], hT, BhT, T, ev, pre=({0: pre} if pre is not None else None))
            tsc("dve", pd[:, 1:4, 0:2], pd[:, 0:3, 256:258], flag[:, 0:1], ALU.mult, [Bpad[0], Bflag], [Bpad[0]])
            tsc("dve", pd[:, 0:3, 258:260], pd[:, 1:4, 2:4], flag[:, 0:1], ALU.mult, [Bpad[0], Bflag], [Bpad[0]])
            co = cvo[0]
            cov = co.rearrange("p (s t) -> p s t", s=4)
            tsc("dve", cov, pd[:, :, 0:256], convw[:, chunk_idx, 0:1], ALU.mult, [Bpad[0], Bdn], [Bcvo[0]])
            for j in range(1, 5):
                stt(cov, pd[:, :, j:j + 256], convw[:, chunk_idx, j:j + 1], cov, ALU.mult, ALU.add, [Bpad[0], Bdn, Bcvo[0]], [Bcvo[0]])
            act(dst, co, AF.Silu, [Bcvo[0]], [Bdst])

        def l2n(src, Bsrc, scale):
            bks = [sm_bank(), sm_bank()]
            act(tmpa[0][:], src, AF.Square, [Bsrc], [Btmpa[0]])
            for h2 in range(2):
                mm1(bks[h2], pbank[bks[h2]][:, :], C(ONES), tmpa[0][:, h2 * 512:(h2 + 1) * 512], [Btmpa[0], Bcst])
                act(tmpa[1][:, h2 * 512:(h2 + 1) * 512], pbank[bks[h2]][:, :], AF.Sqrt, [Bpb[bks[h2]]], [Btmpa[1]],
                    bias=epsc[:, 0:1], scale=1.0)
            S.op("dve", lambda e: e.reciprocal(out=tmpa[1][:], in_=tmpa[1][:]), reads=[Btmpa[1]], writes=[Btmpa[1]])
            stt(src, src, scale, tmpa[1][:], ALU.mult, ALU.mult, [Bsrc, Btmpa[1]], [Bsrc])

        def to_tok(srcT, Bsrc, dst, Bdst):
            for t4 in range(2):
                bk = sm_bank()
                for j in range(4):
                    tl = t4 * 4 + j
                    S.op("pe", (lambda e, o=pbank[bk][:, j * 128:(j + 1) * 128], i=srcT[:, tl * 128:(tl + 1) * 128]:
                                e.transpose(o, i, C(IDENT))), reads=[Bsrc, Bcst], writes=[Bpb[bk]] if j == 0 else [])
                Bpb[bk].lw = ("eng", "pe", S.cnt["pe"])
                cp("act", dst[:, t4 * 4:(t4 + 1) * 4, :], pbank[bk][:, :].rearrange("p (j t) -> p j t", j=4), [Bpb[bk]], [Bdst])

        for g in range(NG):
            if KDN < 2:
                break
            bA, bB = sm_bank(), sm_bank()
            for tl in range(NT):
                S.op("pe", lambda e, tl=tl, bA=bA: e.matmul(pbank[bA][:, tl * 64:tl * 64 + 32], C(UF), lg[:, tl, 0:32], start=True, stop=True),
                     reads=[Bab, Bcst], writes=[Bpb[bA]] if tl == 0 else [])
                S.op("pe", lambda e, tl=tl, bA=bA: e.matmul(pbank[bA][:, tl * 64 + 32:tl * 64 + 64], C(UB), lg[:, tl, 32:64], start=True, stop=True),
                     reads=[Bab, Bcst], writes=[])
                S.op("pe", lambda e, tl=tl, bB=bB: e.matmul(pbank[bB][:, tl * 64:tl * 64 + 64], C(ONES), lg[:, tl, :], start=True, stop=True),
                     reads=[Bab, Bcst], writes=[Bpb[bB]] if tl == 0 else [])
            Bpb[bA].lw = Bpb[bB].lw = ("eng", "pe", S.cnt["pe"])
            vgc = pbank[bA][:, :].rearrange("p (t c) -> p t c", t=NT)
            vgl = pbank[bB][:, :].rearrange("p (t c) -> p t c", t=NT)
            for d in range(2):
                c0 = d * 32 + 2 * g
                qs = slice(2 * d, 2 * d + 2)
                cp("dve", gcg[:, :, qs], vgc[:, :, c0:c0 + 2], [Bpb[bA]], [Bgd])
                tsc("dve", ngcg[:, :, qs], vgc[:, :, c0:c0 + 2], -1.0, ALU.mult, [Bpb[bA]], [Bgd])
                act(edlg[:, :, qs], vgc[:, :, c0:c0 + 2], AF.Exp, [Bpb[bA]], [Bgd])
                tt("dve", kbsg[:, :, qs], edlg[:, :, qs], beta[:, :, c0:c0 + 2], ALU.mult, [Bgd, Bab], [Bgd])
                act(eglg[:, :, qs], vgl[:, :, c0:c0 + 2], AF.Exp, [Bpb[bB]], [Bgd])
                tt("dve", edlg[:, :, qs], vgl[:, :, c0:c0 + 2], gcg[:, :, qs], ALU.subtract, [Bpb[bB], Bgd], [Bgd])
                act(edlg[:, :, qs], edlg[:, :, qs], AF.Exp, [Bgd], [Bgd])
            pq = prefetch(dnwin_d, 0, KC, g)
            pk = prefetch(dnwin_d, 0, KC, 16 + g)
            conv_chunk(g, qT, Bq, pre=pq)
            pv = [prefetch(dnwin_d, 0, KC, 32 + 2 * g), None]
            l2n(qT, Bq, 128.0 ** -0.5)
            conv_chunk(16 + g, kT, Bk, pre=pk)
            pz = [prefetch(dnwin_d, 0, KC, 64 + 2 * g), None]
            l2n(kT, Bk, 1.0)
            to_tok(kT, Bk, ktok, Bktok)
            for hh in range(2):
                if KDN < 3:
                    break
                h = 2 * g + hh
                conv_chunk(32 + h, tmpa[0][:], Btmpa[0], pre=pv[hh])
                if hh == 0 and KDN >= 3:
                    pv[1] = prefetch(dnwin_d, 0, KC, 32 + h + 1)
                to_tok(tmpa[0], Btmpa[0], vtok, Bvtok)

                def evz(ci, c, tb, bk, ps):
                    act(zg[:, tb * 512:(tb + 1) * 512], ps, AF.Silu, [Bpb[bk]], [Bz])
                proj(dnwin_d, 0, KC, [64 + h], hT, BhT, T, evz, pre=({0: pz[hh]} if pz[hh] is not None else None))
                if hh == 0:
                    pz[1] = prefetch(dnwin_d, 0, KC, 64 + h + 1)
                tsc("dve", zg, zg, dnnw[:, 0:1], ALU.mult, [Bz, Bdn], [Bz])
                def tile_gen(d, X, Sst, BS, stt_, step, tl, hh=hh, h=h):
                    ci_ = d * 32 + h
                    q_ = d * 2 + hh
                    MKi = 5 if d == 0 else 7
                    Ui = UF if d == 0 else UB
                    ts_ = slice(tl * 128, (tl + 1) * 128)
                    col = lambda a: a[:, tl, ci_:ci_ + 1]
                    colg = lambda a: a[:, tl, q_:q_ + 1]
                    bk = sm_bank()
                    mm1(bk, pbank[bk][:, 0:128], kT[:, ts_], kT[:, ts_], [Bk])
                    S.op("pe", lambda e, bk=bk: e.matmul(pbank[bk][:, 128:256], kT[:, ts_], qT[:, ts_], start=True, stop=True),
                         reads=[Bk, Bq], writes=[])
                    Bpb[bk].lw = ("eng", "pe", S.cnt["pe"])
                    tsc("pool", X["lgbc"], C(ONES), col(lg), ALU.mult, [Bcst, Bab], [X["Blgbc"]], s2=1.0, op1=ALU.mult)
                    cp("act", X["KK"], pbank[bk][:, 0:128], [Bpb[bk]], [X["BKK"]])
                    cp("act", X["QK"], pbank[bk][:, 128:256], [Bpb[bk]], [X["BQK"]])
                    yield
                    gb = sm_bank()
                    mm1(gb, pbank[gb][:, 0:128], X["lgbc"], C(Ui), [X["Blgbc"], Bcst])
                    tt("dve", X["DT"], pbank[gb][:, 0:128], C(MKi), ALU.add, [Bpb[gb], Bcst], [X["BDT"]])
                    tt("dve", X["Dm"], pbank[gb][:, 0:128], C(MKi + 1), ALU.add, [Bpb[gb], Bcst], [X["BDm"]])
                    act(X["EG"], pbank[gb][:, 0:128], AF.Exp, [Bpb[gb]], [X["BEG"]])
                    act(X["DT"], X["DT"], AF.Exp, [X["BDT"], Bgd], [X["BDT"]], bias=colg(ngcg), scale=1.0)
                    act(X["Dm"], X["Dm"], AF.Exp, [X["BDm"], Bgd], [X["BDm"]], bias=colg(gcg), scale=-1.0)
                    yield
                    tt("pool", X["Dm"], X["Dm"], C(IDENT), ALU.subtract, [X["BDm"], Bcst], [X["BDm"]])
                    stt(X["A"], X["KK"], col(beta), X["Dm"], ALU.mult, ALU.mult, [X["BKK"], Bab, X["BDm"]], [X["BA"]])
                    tt("pool", X["A16"], X["A"], C(BD16), ALU.mult, [X["BA"], Bcst], [X["BA16"]])
                    tt("pool", X["inT"], X["QK"], X["DT"], ALU.mult, [X["BQK"], X["BDT"]], [X["BinT"]])
                    tt("pool", X["qd"], qT[:, ts_], X["EG"], ALU.mult, [Bq, X["BEG"]], [X["Bqd"]])
                    yield
                    bk = sm_bank()
                    tr(bk, pbank[bk][:, 0:128], X["A16"], [X["BA16"]])
                    cp("act", X["Bm"], pbank[bk][:, 0:128], [Bpb[bk]], [X["BBm"]])
                    tt("pool", X["Y"], C(IDENT), X["Bm"], ALU.subtract, [Bcst, X["BBm"]], [X["BY"]])
                    yield
                    b1 = sm_bank()
                    b1a = sm_bank()
                    mm1(b1, pbank[b1][:, 0:128], X["A16"], X["Bm"], [X["BA16"], X["BBm"]])
                    mm1(b1a, pbank[b1a][:, 0:128], X["Bm"], X["A16"], [X["BA16"], X["BBm"]])
                    cp("act", X["Pb"], pbank[b1][:, 0:128], [Bpb[b1]], [X["BPb"]])
                    cp("dve", X["Pa"], pbank[b1a][:, 0:128], [Bpb[b1a]], [X["BPa"]])
                    yield
                    pa, pb2, pan, pbn = "Pa", "Pb", "Pa2", "Pb2"
                    for s_ in range(1, 4):
                        b2 = sm_bank()
                        if s_ < 3:
                            b2z = sm_bank()
                            mm1(b2z, pbank[b2z][:, 0:128], X[pa], X["Y"], [X["B" + pa], X["BY"]])
                            mm1(b2, pbank[b2][:, 0:128], X[pa], X[pb2], [X["B" + pa], X["B" + pb2]])
                            S.op("pe", lambda e, b2=b2, pa=pa, pb2=pb2: e.matmul(pbank[b2][:, 128:256], X[pb2], X[pa], start=True, stop=True),
                                 reads=[X["B" + pa], X["B" + pb2]], writes=[])
                            Bpb[b2].lw = ("eng", "pe", S.cnt["pe"])
                            tt("dve", X["Y"], X["Y"], pbank[b2z][:, 0:128], ALU.add, [X["BY"], Bpb[b2z]], [X["BY"]])
                            cp("act", X[pbn], pbank[b2][:, 0:128], [Bpb[b2]], [X["B" + pbn]])
                            cp("act", X[pan], pbank[b2][:, 128:256], [Bpb[b2]], [X["B" + pan]])
                            pa, pb2, pan, pbn = pan, pbn, pa, pb2
                        else:
                            mm1(b2, pbank[b2][:, 0:128], X[pa], X["Y"], [X["B" + pa], X["BY"]])
                            tt("dve", X["Y"], X["Y"], pbank[b2][:, 0:128], ALU.add, [X["BY"], Bpb[b2]], [X["BY"]])
                        yield
                    tsc("pool", X["vb"], vtok[:, tl, :], col(beta), ALU.mult, [Bvtok, Bab], [X["Bvb"]], s2=1.0, op1=ALU.mult)
                    tsc("pool", X["kbg"], ktok[:, tl, :], colg(kbsg), ALU.mult, [Bktok, Bgd], [X["Bkbg"]], s2=1.0, op1=ALU.mult)
                    tsc("pool", X["kdec"], ktok[:, tl, :], colg(edlg), ALU.mult, [Bktok, Bgd], [X["Bkdec"]], s2=1.0, op1=ALU.mult)
                    for lv in range(3):
                        mi = (ML0 + lv) if d == 0 else (ML0 + 3 + lv)
                        tt("pool", X["Lb"], X["A"], C(mi), ALU.mult, [X["BA"], Bcst], [X["BLb"]])
                        b7 = sm_bank()
                        mm1(b7, pbank[b7][:, 0:128], X["Lb"], X["Y"], [X["BLb"], X["BY"]])
                        b8 = sm_bank()
                        tr(b8, pbank[b8][:, 0:128], X["Y"], [X["BY"]])
                        cp("act", X["Mt"], pbank[b7][:, 0:128], [Bpb[b7]], [X["BMt"]])
                        cp("act", X["V"], pbank[b8][:, 0:128], [Bpb[b8]], [X["BV"]])
                        yield
                        b9 = sm_bank()
                        mm1(b9, pbank[b9][:, 0:128], X["V"], X["Mt"], [X["BV"], X["BMt"]])
                        tt("dve", X["Y"], X["Y"], pbank[b9][:, 0:128], ALU.subtract, [X["BY"], Bpb[b9]], [X["BY"]])
                        yield
                    b3 = sm_bank()
                    mm1(b3, pbank[b3][:, 0:128], X["kbg"], X["Y"], [X["Bkbg"], X["BY"]])
                    act(X["nuT"], pbank[b3][:, 0:128], AF.Identity, [Bpb[b3]], [X["BnuT"]], scale=-1.0)
                    yield
                    Scur = stt_["Scur"]
                    if step > 0 and step % 2 == 0:
                        tsc("dve", Sst[Scur], Sst[Scur], flag[:, 0:1], ALU.mult, [BS[Scur], Bflag], [BS[Scur]])
                    b4 = sm_bank()
                    mmg(b4, pbank[b4][:, 0:128], [(X["Y"], X["vb"], [X["BY"], X["Bvb"]]),
                                                  (X["nuT"], Sst[Scur], [X["BnuT"], BS[Scur]])])
                    cp("act", X["vnew"], pbank[b4][:, 0:128], [Bpb[b4]], [X["Bvnew"]])
                    yield
                    b5 = sm_bank()
                    mmg(b5, pbank[b5][:, 0:128], [(Sst[Scur], X["qd"], [BS[Scur], X["Bqd"]]),
                                                  (X["vnew"], X["inT"], [X["Bvnew"], X["BinT"]])])
                    b6 = sm_bank()
                    mm1(b6, pbank[b6][:, 0:128], X["kdec"], X["vnew"], [X["Bkdec"], X["Bvnew"]])
                    tt("dve", oTf[:, ts_], oTf[:, ts_], pbank[b5][:, 0:128], ALU.add, [BoTf, Bpb[b5]], [BoTf])
                    Snx = Scur ^ 1
                    stt(Sst[Snx], Sst[Scur], colg(eglg), pbank[b6][:, 0:128], ALU.mult, ALU.add, [BS[Scur], Bgd, Bpb[b6]], [BS[Snx]])
                    stt_["Scur"] = Snx
                    if step % 2 == 1:
                        seg = tl // 2
                        S.dma("sp", ns_d[d][seg, h], Sst[Snx], reads=[BS[Snx]], final=True)
                    yield

                if KDN >= 4:
                    if os.environ.get("KSCN") and S.limit is None:
                        S.limit = int(os.environ["KSCN"])
                    S.op("pool", lambda e: e.memset(oTf, 0.0), reads=[], writes=[BoTf])
                    chs = []
                    for d in range(2):
                        S.dma("sp", SS[d][0], s0_d[d][h], writes=[BSS[d][0]])
                        chs.append({"d": d, "order": list(range(NT)) if d == 0 else list(range(NT - 1, -1, -1)),
                                    "next": 0, "active": [], "st": {"Scur": 0}})
                    LAG = 9
                    while any(c_["next"] < NT or c_["active"] for c_ in chs):
                        for c_ in chs:
                            if c_["next"] < NT and len(c_["active"]) < 2 and (not c_["active"] or c_["active"][-1]["n"] >= LAG):
                                step = c_["next"]
                                c_["next"] += 1
                                Xs = XS[c_["d"] * 2 + step % 2]
                                c_["active"].append({"g": tile_gen(c_["d"], Xs, SS[c_["d"]], BSS[c_["d"]], c_["st"], step, c_["order"][step]), "n": 0})
                            for a_ in list(c_["active"]):
                                try:
                                    next(a_["g"])
                                    a_["n"] += 1
                                except StopIteration:
                                    c_["active"].remove(a_)
                if KDN < 6:
                    continue
                bks = [sm_bank(), sm_bank()]
                act(tmpa[0][:], oTf, AF.Square, [BoTf], [Btmpa[0]])
                for h2 in range(2):
                    mm1(bks[h2], pbank[bks[h2]][:, :], C(ONES), tmpa[0][:, h2 * 512:(h2 + 1) * 512], [Btmpa[0], Bcst])
                    act(tmpa[1][:, h2 * 512:(h2 + 1) * 512], pbank[bks[h2]][:, :], AF.Sqrt, [Bpb[bks[h2]]], [Btmpa[1]],
                        bias=epsc[:, 0:1], scale=1.0 / 128.0)
                S.op("dve", lambda e: e.reciprocal(out=tmpa[1][:], in_=tmpa[1][:]), reads=[Btmpa[1]], writes=[Btmpa[1]])
                tt("dve", tmpa[1][:], tmpa[1][:], zg, ALU.mult, [Btmpa[1], Bz], [Btmpa[1]])
                tt("dve", og[:, hh, :], oTf, tmpa[1][:], ALU.mult, [BoTf, Btmpa[1]], [Bog[hh]])

            def evo(ci, c, tb, bk, ps):
                xs = xT[:, c, tb * 512:(tb + 1) * 512]
                stt(xs, ps, mod[0][:, 2 * 16 + c:2 * 16 + c + 1], xs, ALU.mult, ALU.add, [Bpb[bk], Bmod[0], BxT[c]], [BxT[c]])
            if KDN >= 6:
                proj(dnwout_d, 2 * g, 2, list(range(KC)), og, Bog, T, evo)
        S.limit = None
        S.fence()
        print("DN region max cols", rst.get("max"), "of", RW)
    dump()

    rreset()
    if STAGE >= 3:
        uT = ralloc([128, 16, T], BF16)
        BuT = [buf("uT%d" % i) for i in range(16)]
        ffn(0, uT, BuT)
        S.fence()
    dump()

    if STAGE >= 4:
        adaln(1)
        norm_mod(1, nmw[:, 1, :], 0, 1)
    S.fence()
    rreset()
    if STAGE >= 4:
        def rs2(name, shape, dt=F32):
            return ralloc(shape, dt)
        cmbin = rs2("cmbin", [128, 64])
        cmlnw = rs2("cmlnw", [128, 32])
        cmlnb = rs2("cmlnb", [128, 32])
        wsT = rs2("wsT", [128, 16, 128], BF16)
        wsn = [rs2("wsn%d" % i, [128, 128]) for i in range(2)]
        bsr = rs2("bsr", [128, 128])
        uTc = rs2("uTc", [128, 2, T])
        vln = rs2("vln", [128, 2, T])
        vlt = rs2("vlt", [128, NT, 256], BF16)
        mT = rs2("mT", [128, 2, T], BF16)
        mu = rs2("mu", [128, T])
        gtmp = [rs2("gtmp%d" % i, [128, 512]) for i in range(2)]
        gsq = [rs2("gsq%d" % i, [128, 512]) for i in range(2)]
        Bcm, BwsT, Bbsr, BuTc, Bvln, Bvlt, Bmu = buf("cmsmall"), buf("wsT"), buf("bsr"), buf("uTc"), buf("vln"), buf("vlt"), buf("mu")
        Bwsn = [buf("wsn0"), buf("wsn1")]
        BmT = [buf("mT0"), buf("mT1")]
        Bgtmp = [buf("gtmp0"), buf("gtmp1")]
        Bgsq = [buf("gsq0"), buf("gsq1")]
        S.dma("sp", cmbin[:], cmbin_d, writes=[Bcm])
        S.dma("sp", cmlnw[:], cmlnw_d, writes=[Bcm])
        S.dma("sp", cmlnb[:], cmlnb_d, writes=[Bcm])
        S.op("dve", lambda e: e.memset(bsr, 0.0), reads=[], writes=[Bbsr])
        for g in range(16):
            S.dma("sp", wsn[g % 2], cmws_d[g], writes=[Bwsn[g % 2]])
            bk = sm_bank()
            tr(bk, pbank[bk][:, 0:128], wsn[g % 2], [Bwsn[g % 2]])
            cp("act", wsT[:, g, :], pbank[bk][:, 0:128], [Bpb[bk]], [BwsT])
        bsum = [6, 7]
        bsq = [4, 5]
        st["sm"] = 5
        cnt_v = [0]

        def evs(ci, c, tb, bk, ps):
            i2 = cnt_v[0] % 2
            first = cnt_v[0] < 2
            last = cnt_v[0] >= 62
            cnt_v[0] += 1
            act(gtmp[i2], ps, AF.Gelu_apprx_tanh, [Bpb[bk], Bcm], [Bgtmp[i2]], bias=cmbin[:, c:c + 1])
            tt("dve", gsq[i2], gtmp[i2], gtmp[i2], ALU.mult, [Bgtmp[i2]], [Bgsq[i2]])
            S.op("pe", (lambda e, o=pbank[bsum[tb]][:, :], r=gtmp[i2]: e.matmul(o, C(ONES), r, start=first, stop=last)),
                 reads=[Bgtmp[i2], Bcst], writes=[Bpb[bsum[tb]]] if first else [])
            S.op("pe", (lambda e, o=pbank[bsq[tb]][:, :], r=gsq[i2]: e.matmul(o, C(ONES), r, start=first, stop=last)),
                 reads=[Bgsq[i2], Bcst], writes=[Bpb[bsq[tb]]] if first else [])
        proj(cmwin_d, 0, KC, list(range(32, 64)), hT, BhT, T, evs)
        for h2 in range(2):
            Bpb[bsum[h2]].lw = Bpb[bsq[h2]].lw = ("eng", "pe", S.cnt["pe"])
        for h2 in range(2):
            hs = slice(h2 * 512, (h2 + 1) * 512)
            tsc("dve", mu[:, hs], pbank[bsum[h2]][:, :], 1.0 / 4096.0, ALU.mult, [Bpb[bsum[h2]]], [Bmu])
            tt("dve", tmpa[0][:, hs], mu[:, hs], mu[:, hs], ALU.mult, [Bmu], [Btmpa[0]])
            stt(tmpa[0][:, hs], pbank[bsq[h2]][:, :], 1.0 / 4096.0, tmpa[0][:, hs], ALU.mult, ALU.subtract, [Bpb[bsq[h2]], Btmpa[0]], [Btmpa[0]])
            act(rstd[:, hs], tmpa[0][:, hs], AF.Sqrt, [Btmpa[0]], [Brstd], bias=epsc[:, 0:1], scale=1.0)
        S.op("dve", lambda e: e.reciprocal(out=rstd[:], in_=rstd[:]), reads=[Brstd], writes=[Brstd])
        for g in range(16):
            S.dma("sp", bsr[0:1, :], cmbs_d[0:1, g * 128:(g + 1) * 128], writes=[Bbsr])

            def evu(ci, c, tb, bk, ps):
                act(uTc[:, ci, tb * 512:(tb + 1) * 512], ps, AF.Gelu_apprx_tanh, [Bpb[bk], Bcm], [BuTc], bias=cmbin[:, c:c + 1])
            proj(cmwin_d, 0, KC, [2 * g, 2 * g + 1], hT, BhT, T, evu)

            def evv(ci, c, tb, bk, ps):
                act(vln[:, ci, tb * 512:(tb + 1) * 512], ps, AF.Gelu_apprx_tanh, [Bpb[bk], Bcm], [Bvln], bias=cmbin[:, c:c + 1])
            proj(cmwin_d, 0, KC, [32 + 2 * g, 32 + 2 * g + 1], hT, BhT, T, evv)
            for fc in range(2):
                c = 2 * g + fc
                tt("dve", vln[:, fc, :], vln[:, fc, :], mu[:], ALU.subtract, [Bvln, Bmu], [Bvln])
                tt("dve", vln[:, fc, :], vln[:, fc, :], rstd[:], ALU.mult, [Bvln, Brstd], [Bvln])
                tsc("dve", vln[:, fc, :], vln[:, fc, :], cmlnw[:, c:c + 1], ALU.mult, [Bvln, Bcm], [Bvln], s2=cmlnb[:, c:c + 1], op1=ALU.add)
                for t4 in range(2):
                    bk = sm_bank()
                    for j in range(4):
                        tl = t4 * 4 + j
                        S.op("pe", (lambda e, o=pbank[bk][:, j * 128:(j + 1) * 128], i=vln[:, fc, tl * 128:(tl + 1) * 128]:
                                    e.transpose(o, i, C(IDENT))), reads=[Bvln, Bcst], writes=[Bpb[bk]] if j == 0 else [])
                    Bpb[bk].lw = ("eng", "pe", S.cnt["pe"])
                    cp("act", vlt[:, t4 * 4:(t4 + 1) * 4, fc * 128:(fc + 1) * 128],
                       pbank[bk][:, :].rearrange("p (j t) -> p j t", j=4), [Bpb[bk]], [Bvlt])
            for fc in range(2):
                for t4 in range(2):
                    bk = sm_bank()
                    for j in range(4):
                        tl = t4 * 4 + j
                        o = pbank[bk][:, j * 128:(j + 1) * 128]
                        S.op("pe", (lambda e, o=o, tl=tl, fc=fc, g=g: e.matmul(o, vlt[:, tl, fc * 128:(fc + 1) * 128], wsT[:, g, :], start=True, stop=False)),
                             reads=[Bvlt, BwsT], writes=[Bpb[bk]] if j == 0 else [])
                        S.op("pe", (lambda e, o=o: e.matmul(o, C(ONES), bsr, start=False, stop=True)),
                             reads=[Bcst, Bbsr], writes=[])
                    Bpb[bk].lw = ("eng", "pe", S.cnt["pe"])
                    tt("dve", mT[:, fc, t4 * 512:(t4 + 1) * 512], pbank[bk][:, :], uTc[:, fc, t4 * 512:(t4 + 1) * 512], ALU.mult,
                       [Bpb[bk], BuTc], [BmT[fc]])

            def evo2(ci, c, tb, bk, ps):
                xs = xT[:, c, tb * 512:(tb + 1) * 512]
                stt(xs, ps, mod[1][:, 2 * 16 + c:2 * 16 + c + 1], xs, ALU.mult, ALU.add, [Bpb[bk], Bmod[1], BxT[c]], [BxT[c]])
            proj(cmwout_d, 2 * g, 2, list(range(KC)), mT, BmT, T, evo2)
        S.fence()
    dump()

    rreset()
    if STAGE >= 5:
        uT = ralloc([128, 16, T], BF16)
        BuT = [buf("uT1_%d" % i) for i in range(16)]
        ffn(1, uT, BuT)
        S.fence()
    dump()

    rms_stats()
    S.fence()
    rreset()
    yo = [ralloc([128, D]) for i in range(2)]
    Byo = [buf("yo0"), buf("yo1")]
    for c in range(KC):
        stt(xT[:, c, :], xT[:, c, :], fnw[:, c:c + 1], rstd[:], ALU.mult, ALU.mult, [BxT[c], Bsm, Brstd], [BxT[c]])
    for tl in range(NT):
        yb = tl % 2
        for c4 in range(4):
            bk = sm_bank()
            for j in range(4):
                c = c4 * 4 + j
                S.op("pe", (lambda e, o=pbank[bk][:, j * 128:(j + 1) * 128], i=xT[:, c, tl * 128:(tl + 1) * 128]:
                            e.transpose(o, i, C(IDENT))), reads=[BxT[c], Bcst], writes=[Bpb[bk]] if j == 0 else [])
            Bpb[bk].lw = ("eng", "pe", S.cnt["pe"])
            cp("act" if c4 % 2 == 0 else "dve", yo[yb][:, c4 * 512:(c4 + 1) * 512], pbank[bk][:, :], [Bpb[bk]], [Byo[yb]])
        S.dma("sp", y_d[tl * 128:(tl + 1) * 128, :], yo[yb][:], reads=[Byo[yb]], final=True)

    with ExitStack() as stk:
        S.emit(stk)
    return nc


_NC = None


def _fm(v, n):
    return np.ascontiguousarray(np.asarray(v, np.float32).reshape(n, 128).T)


def kernel(x_prompt, x_sample, state_dn_fwd, state_dn_bwd, c, c_ctx, norm_mix_w, norm_mlp_w,
           w_ada, b_ada, dn_w_in, dn_conv_w, dn_A_log, dn_dt_bias, dn_norm_w, dn_w_out,
           cm_w_in, cm_b_in, cm_ln_w, cm_ln_b, cm_w_s, cm_b_s, cm_w_out, w_ff1, w_ff2, final_norm_w):
    global _NC
    if _NC is None:
        _NC = build()
    nc = _NC
    f32 = np.float32
    A = lambda a: np.ascontiguousarray(np.asarray(a, f32))
    r = np.arange(128)
    ident = np.eye(128, dtype=f32)
    ones = np.ones((128, 128), f32)
    Uf = (r[:, None] <= r[None, :]).astype(f32)
    Ub = (r[:, None] >= r[None, :]).astype(f32)
    BIG = 1e30
    M1f = np.where(r[:, None] > r[None, :], -BIG, 0).astype(f32)
    M2f = np.where(r[None, :] > r[:, None], BIG, 0).astype(f32)
    M1b = np.where(r[:, None] < r[None, :], -BIG, 0).astype(f32)
    M2b = np.where(r[None, :] < r[:, None], BIG, 0).astype(f32)
    bd16 = (r[:, None] // 16 == r[None, :] // 16).astype(f32)
    mls = []
    for b_ in (16, 32, 64):
        same = (r[:, None] // (2 * b_) == r[None, :] // (2 * b_))
        mls.append((same & (r[:, None] % (2 * b_) >= b_) & (r[None, :] % (2 * b_) < b_)).astype(f32))
    mlb = [m_.T.copy() for m_ in mls]
    cst = np.stack([ident, ones, Uf, Ub, bd16, M1f, M2f, M1b, M2b] + mls + mlb, axis=1)
    posc = np.zeros((128, 84), f32)
    posc[:, 0:4] = np.arange(4)[None, :] * 128 + r[:, None]
    posc[:, 4:20] = np.arange(16)[None, :]
    posc[:, 20:84] = np.arange(64)[None, :]
    shared = {
        "cst": np.ascontiguousarray(cst), "posc": posc,
        "nmw": np.stack([_fm(norm_mix_w[l], 16) for l in range(2)]),
        "nlw": np.stack([_fm(norm_mlp_w[l], 16) for l in range(2)]),
        "fnw": _fm(final_norm_w, 16),
        "bada": np.stack([_fm(b_ada[l], 96) for l in range(2)]),
        "w_ada": A(w_ada), "dn_w_in": A(dn_w_in[0]),
        "convw": np.ascontiguousarray(np.asarray(dn_conv_w[0], f32).reshape(5, 64, 128).transpose(2, 1, 0)),
        "alog": np.ascontiguousarray(np.broadcast_to(np.asarray(dn_A_log[0], f32).reshape(1, 64), (128, 64))),
        "dtb": np.ascontiguousarray(np.broadcast_to(np.asarray(dn_dt_bias[0], f32).reshape(1, 64), (128, 64))),
        "dnnw": A(dn_norm_w[0]).reshape(128, 1),
        "dn_w_out": A(dn_w_out[0]), "cm_w_in": A(cm_w_in[0]),
        "cmbin": _fm(cm_b_in[0], 64), "cmlnw": _fm(cm_ln_w[0], 32), "cmlnb": _fm(cm_ln_b[0], 32),
        "cm_w_s": A(cm_w_s[0]), "cmbs": A(cm_b_s[0]).reshape(1, 2048),
        "cm_w_out": A(cm_w_out[0]), "w_ff1": A(w_ff1), "w_ff2": A(w_ff2),
    }
    xp = np.asarray(x_prompt, f32)
    xs = np.asarray(x_sample, f32)
    zeros_s = np.zeros((32, 128, 128), f32)
    in_maps = []
    roles = [("ctx", 0), ("ctx", 1), ("ctx", 2), ("ctx", 3), ("lat", 0), ("lat", 1), ("ctx", 0), ("ctx", 1)]
    for kind, i in roles:
        m = {}
        if kind == "ctx":
            m["x"] = np.ascontiguousarray(xp[4 * i:4 * i + 4].reshape(T, D))
            m["flag"] = np.zeros((128, 1), f32)
            m["cond"] = _fm(c_ctx, 16)
            m["s0f"] = zeros_s
            m["s0b"] = zeros_s
        else:
            m["x"] = np.ascontiguousarray(xs[i])
            m["flag"] = np.ones((128, 1), f32)
            m["cond"] = _fm(c[i], 16)
            m["s0f"] = A(state_dn_fwd[i, 0])
            m["s0b"] = A(state_dn_bwd[i, 0])
        m.update(shared)
        in_maps.append({"i_" + k: v for k, v in m.items()})
    sel = os.environ.get("KCORES")
    if sel:
        idx = [int(v) for v in sel.split(",")]
        res = run_bass_kernel_spmd(nc, [in_maps[i] for i in idx], core_ids=list(range(len(idx))))
        R = [None] * 8
        for j, i in enumerate(idx):
            R[i] = res.results[j]
        for i in range(8):
            if R[i] is None:
                R[i] = res.results[0]
    else:
        res = run_bass_kernel_spmd(nc, in_maps, core_ids=list(range(8)))
        R = res.results
    y_prompt = np.concatenate([R[i]["o_y"].reshape(4, 256, D) for i in range(4)], axis=0)
    y_sample = np.stack([R[4]["o_y"], R[5]["o_y"]], axis=0)
    nsf = np.concatenate([R[i]["o_nsf"] for i in range(4)], axis=0)[:, None]
    nsb = np.concatenate([R[i]["o_nsb"] for i in range(4)], axis=0)[:, None]
    if DBG:
        kernel.dbg = [R[i]["o_dbg"] for i in range(8)]
    return (y_prompt.astype(f32), y_sample.astype(f32), nsf.astype(f32), nsb.astype(f32))
```

```python
import math
import os
from contextlib import ExitStack
import numpy as np
import concourse.bass as bass
import concourse.mybir as mybir
from concourse.bass_utils import run_bass_kernel_spmd

F32 = mybir.dt.float32
BF16 = mybir.dt.bfloat16
AF = mybir.ActivationFunctionType
ALU = mybir.AluOpType
ENGS = ("pe", "act", "dve", "pool", "sp")

D = 2048
T = 1024
NT = 8
NG = int(os.environ.get('KNG', '16'))
KC = 16
EPS = 1e-6
import os
DBG = bool(os.environ.get('KDBG'))
STAGE = int(os.environ.get('KSTAGE', '9'))
KDN = int(os.environ.get('KDN', '9'))


class Buf:
    __slots__ = ("name", "lw", "rd", "excl")

    def __init__(self, name, excl=False):
        self.name = name
        self.lw = None
        self.rd = []
        self.excl = excl


class Sched:
    def __init__(self, nc):
        self.nc = nc
        self.ops = {e: [] for e in ENGS}
        self.cnt = {e: 0 for e in ENGS}
        self.dma_sems = []
        self.waited = {e: {} for e in ENGS}
        self.final_dma = []
        self.live_dma = []
        self.pool = {}
        self.pool_n = {}
        self.pool_last = {}

    def _deps(self, reads, writes):
        deps = []
        for b in reads:
            if b.lw is not None:
                deps.append(b.lw)
            if b.excl:
                deps.extend(b.rd)
        for b in writes:
            if b.lw is not None:
                deps.append(b.lw)
            deps.extend(b.rd)
        return deps

    def _resolve(self, eng, deps):
        w = {}
        for d in deps:
            if d[0] == "eng":
                _, e, c = d
                if e == eng and eng in ("pe", "sp", "act", "pool"):
                    continue
                key, val = ("eng", e), c
            else:
                key, val = ("dma", d[1]), d[2]
            if self.waited[eng].get(key, 0) >= val:
                continue
            if w.get(key, 0) < val:
                w[key] = val
        for k, v in w.items():
            self.waited[eng][k] = v
        return list(w.items())

    def _mark(self, tag, reads, writes):
        for b in reads:
            if b.excl:
                b.rd = [t for t in b.rd if t[1] != tag[1] or t[0] != tag[0]] + [tag]
            else:
                b.rd.append(tag)
        for b in writes:
            b.lw = tag
            b.rd = []

    limit = None

    def _budget(self):
        if self.limit is None:
            return True
        if self.limit <= 0:
            return False
        self.limit -= 1
        return True

    def op(self, eng, fn, reads=(), writes=()):
        if not self._budget():
            return None
        waits = self._resolve(eng, self._deps(reads, writes))
        self.cnt[eng] += 1
        tag = ("eng", eng, self.cnt[eng])
        self.ops[eng].append(("op", fn, waits))
        self._mark(tag, reads, writes)
        return tag

    def new_dma_sem(self):
        self.dma_sems.append(0)
        return len(self.dma_sems) - 1

    def dma(self, queue, out, in_, reads=(), writes=(), sem=None, final=False):
        if not self._budget():
            return None
        extra = []
        if sem is None:
            K = 12
            pl = self.pool.setdefault(queue, [])
            n = self.pool_n.get(queue, 0)
            self.pool_n[queue] = n + 1
            if len(pl) < K:
                pl.append(self.new_dma_sem())
            sem = pl[n % K]
            if sem in self.pool_last:
                extra = [self.pool_last[sem]]
        waits = self._resolve(queue, self._deps(reads, writes) + extra)
        self.dma_sems[sem] += 16
        tag = ("dma", sem, self.dma_sems[sem])
        self.ops[queue].append(("dma", (out, in_, sem), waits))
        self._mark(tag, reads, writes)
        self.pool_last[sem] = tag
        self.live_dma.append(tag)
        if final:
            self.final_dma.append(tag)
        return tag

    def fence(self):
        deps = [("eng", e, self.cnt[e]) for e in ENGS if e != "sp" and self.cnt[e] > 0]
        deps += self.live_dma
        self.live_dma = []
        waits = self._resolve("sp", deps)
        self.cnt["sp"] += 1
        tag = ("eng", "sp", self.cnt["sp"])
        self.ops["sp"].append(("op", lambda e: e.nop(), waits))
        for e in ENGS:
            if e == "sp":
                continue
            w = self._resolve(e, [tag])
            if w:
                self.ops[e].append(("wait", None, w))

    def emit(self, stack):
        nc = self.nc
        esem = {e: stack.enter_context(nc.semaphore("s_" + e)) for e in ENGS}
        dsem = [stack.enter_context(nc.semaphore("d_%d" % i)) for i in range(len(self.dma_sems))]
        block = stack.enter_context(nc.Block())

        def semof(key):
            return esem[key[1]] if key[0] == "eng" else dsem[key[1]]

        def run(engname, engine):
            for kind, payload, waits in self.ops[engname]:
                for key, val in waits:
                    engine.wait_ge(semof(key), val)
                if kind == "op":
                    payload(engine).then_inc(esem[engname], 1)
                elif kind == "dma":
                    out, in_, si = payload
                    engine.dma_start(out=out, in_=in_).then_inc(dsem[si], 16)
            if engname == "sp":
                for tag in self.final_dma:
                    engine.wait_ge(dsem[tag[1]], tag[2])

        block.sync(lambda e: run("sp", e))
        block.tensor(lambda e: run("pe", e))
        block.scalar(lambda e: run("act", e))
        block.vector(lambda e: run("dve", e))
        block.gpsimd(lambda e: run("pool", e))


def build():
    nc = bass.Bass("TRN2", target_bir_lowering=False)
    S = Sched(nc)

    def din(name, shape):
        return nc.dram_tensor("i_" + name, list(shape), F32, kind="ExternalInput").ap()

    def dout(name, shape):
        return nc.dram_tensor("o_" + name, list(shape), F32, kind="ExternalOutput").ap()

    x_d = din("x", [T, D])
    flag_d = din("flag", [128, 1])
    cond_d = din("cond", [128, KC])
    s0_d = [din("s0f", [32, 128, 128]), din("s0b", [32, 128, 128])]
    cst_d = din("cst", [128, 15, 128])
    posc_d = din("posc", [128, 4 + 16 + 64])
    nmw_d = din("nmw", [2, 128, KC])
    nlw_d = din("nlw", [2, 128, KC])
    fnw_d = din("fnw", [128, KC])
    bada_d = din("bada", [2, 128, 96])
    wada_d = din("w_ada", [2, D, 6 * D])
    dnwin_d = din("dn_w_in", [D, 12416])
    convw_d = din("convw", [128, 64, 5])
    alog_d = din("alog", [128, 64])
    dtb_d = din("dtb", [128, 64])
    dnnw_d = din("dnnw", [128, 1])
    dnwout_d = din("dn_w_out", [4096, D])
    cmwin_d = din("cm_w_in", [D, 8192])
    cmbin_d = din("cmbin", [128, 64])
    cmlnw_d = din("cmlnw", [128, 32])
    cmlnb_d = din("cmlnb", [128, 32])
    cmws_d = din("cm_w_s", [16, 128, 128])
    cmbs_d = din("cmbs", [1, 16 * 128])
    cmwout_d = din("cm_w_out", [4096, D])
    wff1_d = din("w_ff1", [2, D, 8192])
    wff2_d = din("w_ff2", [2, 8192, D])
    y_d = dout("y", [T, D])
    ns_d = [dout("nsf", [4, 32, 128, 128]), dout("nsb", [4, 32, 128, 128])]
    dbg_d = dout("dbg", [8, 128, KC, T]) if DBG else None

    sb = nc.alloc_sbuf_tensor
    xT = sb("xT", [128, KC, T], F32)
    hT = sb("hT", [128, KC, T], BF16)
    NSLOT = 4
    wsl = [sb("wsl%d" % i, [128, 16, 128], BF16) for i in range(NSLOT)]
    cst = sb("cst", [128, 15, 128], F32)
    flag = sb("flag", [128, 1], F32)
    cond = sb("cond", [128, KC], F32)
    scond = sb("scond", [128, KC, 128], BF16)
    mod = [sb("mod%d" % l, [128, 96], F32) for l in range(2)]
    bada = [sb("bada%d" % l, [128, 96], F32) for l in range(2)]
    nmw = sb("nmw", [128, 2, KC], F32)
    nlw = sb("nlw", [128, 2, KC], F32)
    fnw = sb("fnw", [128, KC], F32)
    geff = sb("geff", [128, KC], F32)
    rstd = sb("rstd", [128, T], F32)
    tmpa = [sb("tmpa%d" % i, [128, T], F32) for i in range(2)]

    RW = (nc.sbuf_bytes_remaining - 2560) // 4 // 8 * 8
    print("region cols", RW)
    Rg = sb("Rg", [128, RW], F32)
    rst = {"off": 0}

    def rreset():
        rst["off"] = 0

    def ralloc(shape, dt=F32):
        n = 1
        for v in shape[1:]:
            n *= v
        cols = n if dt == F32 else (n + 1) // 2
        cols = (cols + 7) // 8 * 8
        o = rst["off"]
        rst["off"] += cols
        rst["max"] = max(rst.get("max", 0), rst["off"])
        assert rst["off"] <= RW, ("region overflow", rst["off"])
        a = Rg[0:shape[0], o:o + cols]
        if dt != F32:
            a = a.bitcast(dt)
        a = a[:, 0:n]
        if len(shape) == 3:
            a = a.rearrange("p (a b) -> p a b", a=shape[1])
        elif len(shape) == 4:
            a = a.rearrange("p (a b c) -> p a b c", a=shape[1], b=shape[2])
        return a

    B = {}

    def buf(name):
        if name not in B:
            B[name] = Buf(name)
        return B[name]

    BxT = [buf("xT%d" % c) for c in range(KC)]
    BhT = [buf("hT%d" % c) for c in range(KC)]
    Bw = [buf("wsl%d" % i) for i in range(NSLOT)]
    wsem = [S.new_dma_sem() for _ in range(NSLOT)]
    Bcst, Bflag, Bcond, Bscond = buf("cst"), buf("flag"), buf("cond"), buf("scond")
    Bmod = [buf("mod0"), buf("mod1")]
    Bbada = [buf("bada0"), buf("bada1")]
    Bsm = buf("small")
    Bgeff, Brstd = buf("geff"), buf("rstd")
    Btmpa = [buf("tmpa0"), buf("tmpa1")]

    pbank = [nc.alloc_psum_tensor("pb%d" % i, [128, 512], F32) for i in range(8)]
    Bpb = [Buf("pb%d" % i, excl=True) for i in range(8)]
    st = {"big": 0, "sm": 0, "w": 0, "ta": 0}

    def big_bank():
        st["big"] ^= 1
        return st["big"]

    def sm_bank():
        st["sm"] = (st["sm"] + 1) % 6
        return 2 + st["sm"]

    IDENT, ONES, UF, UB = 0, 1, 2, 3
    BD16, ML0 = 4, 9

    def C(i):
        return cst[:, i, :]

    def mmg(bank, out_ap, terms):
        n = len(terms)
        for i, (l, r, rd) in enumerate(terms):
            S.op("pe", (lambda e, o=out_ap, l=l, r=r, i=i: e.matmul(o, l, r, start=(i == 0), stop=(i == n - 1))),
                 reads=list(rd), writes=[Bpb[bank]] if i == 0 else [])
        Bpb[bank].lw = ("eng", "pe", S.cnt["pe"])

    def mm1(bank, out_ap, l, r, rd):
        S.op("pe", lambda e: e.matmul(out_ap, l, r, start=True, stop=True), reads=list(rd), writes=[Bpb[bank]])

    def tr(bank, out_ap, in_ap, rd):
        S.op("pe", lambda e: e.transpose(out_ap, in_ap, C(IDENT)), reads=list(rd) + [Bcst], writes=[Bpb[bank]])

    def act(out, in_, func, rd, wr, bias=None, scale=None, eng="act"):
        kw = {}
        if bias is not None:
            kw["bias"] = bias
        if scale is not None:
            kw["scale"] = scale
        S.op("act", lambda e: e.activation(out=out, in_=in_, func=func, **kw), reads=rd, writes=wr)

    def tsc(eng, out, in0, s1, op0, rd, wr, s2=None, op1=None):
        if op1 is None:
            S.op(eng, lambda e: e.tensor_scalar(out=out, in0=in0, scalar1=s1, scalar2=None, op0=op0), reads=rd, writes=wr)
        else:
            S.op(eng, lambda e: e.tensor_scalar(out=out, in0=in0, scalar1=s1, scalar2=s2, op0=op0, op1=op1), reads=rd, writes=wr)

    def stt(out, in0, sc, in1, op0, op1, rd, wr):
        S.op("dve", lambda e: e.scalar_tensor_tensor(out=out, in0=in0, scalar=sc, in1=in1, op0=op0, op1=op1), reads=rd, writes=wr)

    def tt(eng, out, in0, in1, op, rd, wr):
        S.op(eng, lambda e: e.tensor_tensor(out=out, in0=in0, in1=in1, op=op), reads=rd, writes=wr)

    def cp(eng, out, in_, rd, wr):
        if eng == "act":
            S.op("act", lambda e: e.copy(out=out, in_=in_), reads=rd, writes=wr)
        else:
            S.op(eng, lambda e: e.tensor_copy(out=out, in_=in_), reads=rd, writes=wr)

    S.dma("sp", cst[:], cst_d, writes=[Bcst])
    S.dma("sp", flag[:], flag_d, writes=[Bflag])
    S.dma("sp", cond[:], cond_d, writes=[Bcond])
    for l in range(2):
        S.dma("sp", bada[l][:], bada_d[l], writes=[Bbada[l]])
        S.dma("sp", nmw[:, l, :], nmw_d[l], writes=[Bsm])
        S.dma("sp", nlw[:, l, :], nlw_d[l], writes=[Bsm])
    S.dma("sp", fnw[:], fnw_d, writes=[Bsm])

    def proj(w_ap, k0, kc_n, col_chunks, actT, act_bufs, tcols, evac, pre=None):
        ntb = (tcols + 511) // 512
        n_ = len(col_chunks)
        PF = NSLOT - 1
        slots = {}

        def issue(ci_):
            sl_ = st["w"] % NSLOT
            st["w"] += 1
            slots[ci_] = sl_
            c_ = col_chunks[ci_]
            src = w_ap[k0 * 128:(k0 + kc_n) * 128, c_ * 128:(c_ + 1) * 128].rearrange("(kc p) j -> p kc j", p=128)
            S.dma("pool", wsl[sl_][:, 0:kc_n, :], src, writes=[Bw[sl_]], sem=wsem[sl_])
        if pre:
            slots.update(pre)
        for ci_ in range(min(PF, n_)):
            if ci_ not in slots:
                issue(ci_)
        for ci, c in enumerate(col_chunks):
            if ci + PF < n_ and (ci + PF) not in slots:
                issue(ci + PF)
            sl = slots[ci]
            for tb in range(ntb):
                t0, t1 = tb * 512, min(tcols, tb * 512 + 512)
                bk = big_bank()
                terms = [(wsl[sl][:, kc, :], actT[:, kc, t0:t1], [Bw[sl], act_bufs[kc]]) for kc in range(kc_n)]
                mmg(bk, pbank[bk][:, 0:t1 - t0], terms)
                evac(ci, c, tb, bk, pbank[bk][:, 0:t1 - t0])

    def prefetch(w_ap, k0, kc_n, c_):
        sl_ = st["w"] % NSLOT
        st["w"] += 1
        src = w_ap[k0 * 128:(k0 + kc_n) * 128, c_ * 128:(c_ + 1) * 128].rearrange("(kc p) j -> p kc j", p=128)
        S.dma("pool", wsl[sl_][:, 0:kc_n, :], src, writes=[Bw[sl_]], sem=wsem[sl_])
        return sl_

    rreset()
    xin = [ralloc([128, D]) for i in range(2)]
    Bxin = [buf("xin0"), buf("xin1")]
    for tl in range(NT):
        xb = tl % 2
        S.dma("sp", xin[xb][:], x_d[tl * 128:(tl + 1) * 128, :], writes=[Bxin[xb]])
        for c4 in range(4):
            bk = sm_bank()
            for j in range(4):
                c = c4 * 4 + j
                S.op("pe", (lambda e, o=pbank[bk][:, j * 128:(j + 1) * 128], i=xin[xb][:, c * 128:(c + 1) * 128]:
                            e.transpose(o, i, C(IDENT))), reads=[Bxin[xb], Bcst], writes=[Bpb[bk]] if j == 0 else [])
            Bpb[bk].lw = ("eng", "pe", S.cnt["pe"])
            eng = "act" if c4 % 2 == 0 else "dve"
            cp(eng, xT[:, c4 * 4:(c4 + 1) * 4, tl * 128:(tl + 1) * 128],
               pbank[bk][:, :].rearrange("p (j t) -> p j t", j=4), [Bpb[bk]], [BxT[c4 * 4 + j] for j in range(4)])

    posc = sb("posc", [128, 84], F32)
    freq = sb("freq", [128, 4], F32)
    ang = sb("ang", [128, 64], F32)
    acc = sb("acc", [128, 64], F32)
    ptab = sb("ptab", [128, 64], F32)
    Bpos = buf("pos")
    S.dma("sp", posc[:], posc_d, writes=[Bpos])
    act(freq[:], posc[:, 0:4], AF.Exp, [Bpos], [Bpos], scale=-math.log(10000.0) / 512.0)
    TWO_PI = 2.0 * math.pi
    for c in range(KC):
        quarter, cq = c // 4, c % 4
        n = 16 if quarter < 2 else 64
        pv = posc[:, 4:20] if quarter < 2 else posc[:, 20:84]
        phase = 0.0 if quarter % 2 == 0 else math.pi / 2
        tsc("dve", ang[:, 0:n], pv, freq[:, cq:cq + 1], ALU.mult, [Bpos], [Bpos], s2=phase, op1=ALU.add)
        S.op("dve", lambda e, n=n: e.memset(acc[:, 0:n], 0.0), reads=[], writes=[Bpos])
        for m in range(1, 11):
            stt(acc[:, 0:n], ang[:, 0:n], TWO_PI * m, acc[:, 0:n], ALU.is_ge, ALU.add, [Bpos], [Bpos])
        stt(ang[:, 0:n], acc[:, 0:n], -TWO_PI, ang[:, 0:n], ALU.mult, ALU.add, [Bpos], [Bpos])
        tsc("dve", ang[:, 0:n], ang[:, 0:n], -1.0, ALU.mult, [Bpos], [Bpos], s2=math.pi, op1=ALU.add)
        act(ptab[:, 0:n], ang[:, 0:n], AF.Sin, [Bpos], [Bpos])
        tsc("dve", ptab[:, 0:n], ptab[:, 0:n], flag[:, 0:1], ALU.mult, [Bpos, Bflag], [Bpos])
        for r_ in range(16):
            xv = xT[:, c, r_ * 64:(r_ + 1) * 64]
            if quarter < 2:
                tsc("pool", xv, xv, ptab[:, r_:r_ + 1], ALU.add, [Bpos, BxT[c]], [BxT[c]])
            else:
                tt("pool", xv, xv, ptab[:, 0:64], ALU.add, [Bpos, BxT[c]], [BxT[c]])

    dbg_i = [0]

    def dump():
        if DBG:
            S.dma("sp", dbg_d[dbg_i[0]], xT[:], reads=BxT, final=True)
            dbg_i[0] += 1

    dump()

    tmpc = sb("tmpc", [128, KC], F32)
    act(tmpc[:], cond[:], AF.Silu, [Bcond], [Bsm])
    for kc_ in range(KC):
        tsc("dve", scond[:, kc_, :], C(ONES), tmpc[:, kc_:kc_ + 1], ALU.mult, [Bsm, Bcst], [Bscond])

    def adaln(l):
        def ev(ci, c, tb, bk, ps):
            tt("dve", mod[l][:, c:c + 1], ps[:, 0:1], bada[l][:, c:c + 1], ALU.add, [Bpb[bk], Bbada[l]], [Bmod[l]])
        proj(wada_d[l], 0, KC, list(range(96)), scond, [Bscond] * KC, 128, ev)

    def rms_stats():
        bks = [sm_bank(), sm_bank()]
        for c in range(KC):
            tb_ = c % 2
            act(tmpa[tb_][:], xT[:, c, :], AF.Square, [BxT[c]], [Btmpa[tb_]])
            for h2 in range(2):
                S.op("pe", (lambda e, o=pbank[bks[h2]][:, :], r=tmpa[tb_][:, h2 * 512:(h2 + 1) * 512], c=c:
                            e.matmul(o, C(ONES), r, start=(c == 0), stop=(c == KC - 1))),
                     reads=[Btmpa[tb_], Bcst], writes=[Bpb[bks[h2]]] if c == 0 else [])
        for h2 in range(2):
            Bpb[bks[h2]].lw = ("eng", "pe", S.cnt["pe"])
        for h2 in range(2):
            act(rstd[:, h2 * 512:(h2 + 1) * 512], pbank[bks[h2]][:, :], AF.Sqrt, [Bpb[bks[h2]]], [Brstd],
                bias=epsc[:, 0:1], scale=1.0 / D)
        S.op("dve", lambda e: e.reciprocal(out=rstd[:], in_=rstd[:]), reads=[Brstd], writes=[Brstd])

    epsc = sb("epsc", [128, 1], F32)
    S.op("dve", lambda e: e.memset(epsc[:], EPS), reads=[], writes=[Bsm])

    def norm_mod(l, wtile, shift_i, scale_i):
        rms_stats()
        stt(geff[:], mod[l][:, scale_i * 16:(scale_i + 1) * 16], 1.0, wtile, ALU.add, ALU.mult, [Bmod[l], Bsm], [Bgeff])
        for c in range(KC):
            tb_ = c % 2
            stt(tmpa[tb_][:], xT[:, c, :], geff[:, c:c + 1], rstd[:], ALU.mult, ALU.mult, [BxT[c], Bgeff, Brstd], [Btmpa[tb_]])
            act(hT[:, c, :], tmpa[tb_][:], AF.Identity, [Btmpa[tb_], Bmod[l]], [BhT[c]],
                bias=mod[l][:, shift_i * 16 + c:shift_i * 16 + c + 1])

    def ffn(l, uT, BuT):
        norm_mod(l, nlw[:, l, :], 3, 4)
        for J in range(4):
            def evA(ci, c, tb, bk, ps):
                e2 = tmpa[0][:, 0:512] if tb == 0 else tmpa[1][:, 0:512]
                be = Btmpa[0] if tb == 0 else Btmpa[1]
                act(e2, ps, AF.Relu, [Bpb[bk]], [be])
                act(uT[:, ci, tb * 512:(tb + 1) * 512], e2, AF.Square, [be], [BuT[ci]])
            proj(wff1_d[l], 0, KC, list(range(J * 16, J * 16 + 16)), hT, BhT, T, evA)

            def evB(ci, c, tb, bk, ps):
                xs = xT[:, c, tb * 512:(tb + 1) * 512]
                stt(xs, ps, mod[l][:, 5 * 16 + c:5 * 16 + c + 1], xs, ALU.mult, ALU.add, [Bpb[bk], Bmod[l], BxT[c]], [BxT[c]])
            proj(wff2_d[l], J * 16, 16, list(range(KC)), uT, BuT, T, evB)

    if STAGE >= 1:
        if not os.environ.get("KSKIP_ADA"):
            adaln(0)
        if not os.environ.get("KSKIP_NORM"):
            norm_mod(0, nmw[:, 0, :], 0, 1)
        if os.environ.get("KDBG_H"):
            for c in range(KC):
                cp("dve", xT[:, c, :], hT[:, c, :], [BhT[c]], [BxT[c]])
            dump()

    S.fence()
    rreset()
    if STAGE >= 2:
        def rs(name, shape, dt=F32):
            return ralloc(shape, dt)
        convw = rs("convw", [128, 64, 5])
        alog = rs("alog", [128, 64])
        dtb = rs("dtb", [128, 64])
        dnnw = rs("dnnw", [128, 1])
        Bdn = buf("dnsmall")
        S.dma("sp", convw[:], convw_d, writes=[Bdn])
        S.dma("sp", alog[:], alog_d, writes=[Bdn])
        S.dma("sp", dtb[:], dtb_d, writes=[Bdn])
        S.dma("sp", dnnw[:], dnnw_d, writes=[Bdn])
        sc_off = rst["off"]
        XS = []
        ALIAS = {"lgbc": 0, "A16": 0, "nuT": 0, "KK": 1, "Pb2": 1, "kdec": 1, "QK": 2, "inT": 2, "DT": 3, "vb": 3,
                 "Dm": 4, "Bm": 4, "kbg": 4, "EG": 5, "qd": 5, "A": 6, "Pa": 7, "Lb": 7, "vnew": 7, "Pb": 8, "Mt": 8,
                 "Pa2": 9, "V": 9, "Y": 10}
        for si_ in range(4):
            phys = [rs("sc%d_%d" % (si_, k_), [128, 128]) for k_ in range(11)]
            pb_ = [buf("sc%d_%d" % (si_, k_)) for k_ in range(11)]
            X = {}
            for nm, k_ in ALIAS.items():
                X[nm] = phys[k_]
                X["B" + nm] = pb_[k_]
            XS.append(X)
        gcg = rs("gcg", [128, NT, 4])
        ngcg = rs("ngcg", [128, NT, 4])
        eglg = rs("eglg", [128, NT, 4])
        edlg = rs("edlg", [128, NT, 4])
        kbsg = rs("kbsg", [128, NT, 4])
        Bgd = buf("gdecay")
        SS = [[rs("Sst%d_%d" % (d_, i), [128, 128]) for i in range(2)] for d_ in range(2)]
        BSS = [[buf("Sst%d_%d" % (d_, i)) for i in range(2)] for d_ in range(2)]
        wab = Rg[:, sc_off:sc_off + 1024].bitcast(BF16).rearrange("p (a b) -> p a b", a=KC)
        Bwab = buf("wab")
        S.dma("pool", wab[:], dnwin_d[:, 12288:12416].rearrange("(kc p) j -> p kc j", p=128), writes=[Bwab])
        lg = rs("lg", [128, NT, 64])
        beta = rs("beta", [128, NT, 64])
        Bab = buf("ab")
        S.op("act", lambda e: e.activation(out=alog[:], in_=alog[:], func=AF.Exp), reads=[Bdn], writes=[Bdn])
        for tl in range(NT):
            bk = sm_bank()
            mmg(bk, pbank[bk][:, 0:128], [(hT[:, kc, tl * 128:(tl + 1) * 128], wab[:, kc, :], [BhT[kc], Bwab]) for kc in range(KC)])
            for d in range(2):
                pa = pbank[bk][:, d * 64:d * 64 + 32]
                pb_ = pbank[bk][:, d * 64 + 32:d * 64 + 64]
                lgs = lg[:, tl, d * 32:(d + 1) * 32]
                tt("dve", lgs, pa, dtb[:, d * 32:(d + 1) * 32], ALU.add, [Bpb[bk], Bdn], [Bab])
                act(lgs, lgs, AF.Exp, [Bab], [Bab])
                act(lgs, lgs, AF.Ln, [Bab], [Bab], bias=1.0)
                stt(lgs, lgs, -1.0, alog[:, d * 32:(d + 1) * 32], ALU.mult, ALU.mult, [Bab, Bdn], [Bab])
                act(beta[:, tl, d * 32:(d + 1) * 32], pb_, AF.Sigmoid, [Bpb[bk]], [Bab])
        S.fence()
        pad = [rs("pad0", [128, 4, 260])]
        Bpad = [buf("pad0")]
        cvo = [rs("cvo0", [128, T])]
        Bcvo = [buf("cvo0")]
        qT = rs("qT", [128, T])
        kT = rs("kT", [128, T])
        zg = rs("zg", [128, T])
        ktok = rs("ktok", [128, NT, 128])
        vtok = rs("vtok", [128, NT, 128])
        oTf = rs("oTf", [128, T])
        og = rs("og", [128, 2, T], BF16)
        Bq, Bk, Bz, Bktok, Bvtok = buf("qT"), buf("kT"), buf("zg"), buf("ktok"), buf("vtok")
        BoTf, Bog = buf("oTf"), [buf("og0"), buf("og1")]
        S.op("pool", lambda e: e.memset(pad[0][:], 0.0), reads=[], writes=Bpad)

        def conv_chunk(chunk_idx, dst, Bdst, pre=None):
            pd = pad[0]

            def ev(ci, c, tb, bk, ps):
                cp("act", pd[:, tb * 2:tb * 2 + 2, 2:258], ps.rearrange("p (s t) -> p s t", s=2), [Bpb[bk]], [Bpad[0]])
            proj(dnwin_d, 0, KC, [chunk_idx], hT, BhT, T, ev, pre=({0: pre} if pre is not None else None))
            tsc("dve", pd[:, 1:4, 0:2], pd[:, 0:3, 256:258], flag[:, 0:1], ALU.mult, [Bpad[0], Bflag], [Bpad[0]])
            tsc("dve", pd[:, 0:3, 258:260], pd[:, 1:4, 2:4], flag[:, 0:1], ALU.mult, [Bpad[0], Bflag], [Bpad[0]])
            co = cvo[0]
            cov = co.rearrange("p (s t) -> p s t", s=4)
            tsc("dve", cov, pd[:, :, 0:256], convw[:, chunk_idx, 0:1], ALU.mult, [Bpad[0], Bdn], [Bcvo[0]])
            for j in range(1, 5):
                stt(cov, pd[:, :, j:j + 256], convw[:, chunk_idx, j:j + 1], cov, ALU.mult, ALU.add, [Bpad[0], Bdn, Bcvo[0]], [Bcvo[0]])
            act(dst, co, AF.Silu, [Bcvo[0]], [Bdst])

        def l2n(src, Bsrc, scale):
            bks = [sm_bank(), sm_bank()]
            act(tmpa[0][:], src, AF.Square, [Bsrc], [Btmpa[0]])
            for h2 in range(2):
                mm1(bks[h2], pbank[bks[h2]][:, :], C(ONES), tmpa[0][:, h2 * 512:(h2 + 1) * 512], [Btmpa[0], Bcst])
                act(tmpa[1][:, h2 * 512:(h2 + 1) * 512], pbank[bks[h2]][:, :], AF.Sqrt, [Bpb[bks[h2]]], [Btmpa[1]],
                    bias=epsc[:, 0:1], scale=1.0)
            S.op("dve", lambda e: e.reciprocal(out=tmpa[1][:], in_=tmpa[1][:]), reads=[Btmpa[1]], writes=[Btmpa[1]])
            stt(src, src, scale, tmpa[1][:], ALU.mult, ALU.mult, [Bsrc, Btmpa[1]], [Bsrc])

        def to_tok(srcT, Bsrc, dst, Bdst):
            for t4 in range(2):
                bk = sm_bank()
                for j in range(4):
                    tl = t4 * 4 + j
                    S.op("pe", (lambda e, o=pbank[bk][:, j * 128:(j + 1) * 128], i=srcT[:, tl * 128:(tl + 1) * 128]:
                                e.transpose(o, i, C(IDENT))), reads=[Bsrc, Bcst], writes=[Bpb[bk]] if j == 0 else [])
                Bpb[bk].lw = ("eng", "pe", S.cnt["pe"])
                cp("act", dst[:, t4 * 4:(t4 + 1) * 4, :], pbank[bk][:, :].rearrange("p (j t) -> p j t", j=4), [Bpb[bk]], [Bdst])

        for g in range(NG):
            if KDN < 2:
                break
            bA, bB = sm_bank(), sm_bank()
            for tl in range(NT):
                S.op("pe", lambda e, tl=tl, bA=bA: e.matmul(pbank[bA][:, tl * 64:tl * 64 + 32], C(UF), lg[:, tl, 0:32], start=True, stop=True),
                     reads=[Bab, Bcst], writes=[Bpb[bA]] if tl == 0 else [])
                S.op("pe", lambda e, tl=tl, bA=bA: e.matmul(pbank[bA][:, tl * 64 + 32:tl * 64 + 64], C(UB), lg[:, tl, 32:64], start=True, stop=True),
                     reads=[Bab, Bcst], writes=[])
                S.op("pe", lambda e, tl=tl, bB=bB: e.matmul(pbank[bB][:, tl * 64:tl * 64 + 64], C(ONES), lg[:, tl, :], start=True, stop=True),
                     reads=[Bab, Bcst], writes=[Bpb[bB]] if tl == 0 else [])
            Bpb[bA].lw = Bpb[bB].lw = ("eng", "pe", S.cnt["pe"])
            vgc = pbank[bA][:, :].rearrange("p (t c) -> p t c", t=NT)
            vgl = pbank[bB][:, :].rearrange("p (t c) -> p t c", t=NT)
            for d in range(2):
                c0 = d * 32 + 2 * g
                qs = slice(2 * d, 2 * d + 2)
                cp("dve", gcg[:, :, qs], vgc[:, :, c0:c0 + 2], [Bpb[bA]], [Bgd])
                tsc("dve", ngcg[:, :, qs], vgc[:, :, c0:c0 + 2], -1.0, ALU.mult, [Bpb[bA]], [Bgd])
                act(edlg[:, :, qs], vgc[:, :, c0:c0 + 2], AF.Exp, [Bpb[bA]], [Bgd])
                tt("dve", kbsg[:, :, qs], edlg[:, :, qs], beta[:, :, c0:c0 + 2], ALU.mult, [Bgd, Bab], [Bgd])
                act(eglg[:, :, qs], vgl[:, :, c0:c0 + 2], AF.Exp, [Bpb[bB]], [Bgd])
                tt("dve", edlg[:, :, qs], vgl[:, :, c0:c0 + 2], gcg[:, :, qs], ALU.subtract, [Bpb[bB], Bgd], [Bgd])
                act(edlg[:, :, qs], edlg[:, :, qs], AF.Exp, [Bgd], [Bgd])
            pq = prefetch(dnwin_d, 0, KC, g)
            pk = prefetch(dnwin_d, 0, KC, 16 + g)
            conv_chunk(g, qT, Bq, pre=pq)
            pv = [prefetch(dnwin_d, 0, KC, 32 + 2 * g), None]
            l2n(qT, Bq, 128.0 ** -0.5)
            conv_chunk(16 + g, kT, Bk, pre=pk)
            pz = [prefetch(dnwin_d, 0, KC, 64 + 2 * g), None]
            l2n(kT, Bk, 1.0)
            to_tok(kT, Bk, ktok, Bktok)
            for hh in range(2):
                if KDN < 3:
                    break
                h = 2 * g + hh
                conv_chunk(32 + h, tmpa[0][:], Btmpa[0], pre=pv[hh])
                if hh == 0 and KDN >= 3:
                    pv[1] = prefetch(dnwin_d, 0, KC, 32 + h + 1)
                to_tok(tmpa[0], Btmpa[0], vtok, Bvtok)

                def evz(ci, c, tb, bk, ps):
                    act(zg[:, tb * 512:(tb + 1) * 512], ps, AF.Silu, [Bpb[bk]], [Bz])
                proj(dnwin_d, 0, KC, [64 + h], hT, BhT, T, evz, pre=({0: pz[hh]} if pz[hh] is not None else None))
                if hh == 0:
                    pz[1] = prefetch(dnwin_d, 0, KC, 64 + h + 1)
                tsc("dve", zg, zg, dnnw[:, 0:1], ALU.mult, [Bz, Bdn], [Bz])
                def tile_gen(d, X, Sst, BS, stt_, step, tl, hh=hh, h=h):
                    ci_ = d * 32 + h
                    q_ = d * 2 + hh
                    MKi = 5 if d == 0 else 7
                    Ui = UF if d == 0 else UB
                    ts_ = slice(tl * 128, (tl + 1) * 128)
                    col = lambda a: a[:, tl, ci_:ci_ + 1]
                    colg = lambda a: a[:, tl, q_:q_ + 1]
                    bk = sm_bank()
                    mm1(bk, pbank[bk][:, 0:128], kT[:, ts_], kT[:, ts_], [Bk])
                    S.op("pe", lambda e, bk=bk: e.matmul(pbank[bk][:, 128:256], kT[:, ts_], qT[:, ts_], start=True, stop=True),
                         reads=[Bk, Bq], writes=[])
                    Bpb[bk].lw = ("eng", "pe", S.cnt["pe"])
                    act(X["lgbc"], C(ONES), AF.Identity, [Bcst, Bab], [X["Blgbc"]], scale=col(lg))
                    cp("act", X["KK"], pbank[bk][:, 0:128], [Bpb[bk]], [X["BKK"]])
                    cp("act", X["QK"], pbank[bk][:, 128:256], [Bpb[bk]], [X["BQK"]])
                    yield
                    gb = sm_bank()
                    mm1(gb, pbank[gb][:, 0:128], X["lgbc"], C(Ui), [X["Blgbc"], Bcst])
                    tt("dve", X["DT"], pbank[gb][:, 0:128], C(MKi), ALU.add, [Bpb[gb], Bcst], [X["BDT"]])
                    tt("dve", X["Dm"], pbank[gb][:, 0:128], C(MKi + 1), ALU.add, [Bpb[gb], Bcst], [X["BDm"]])
                    act(X["EG"], pbank[gb][:, 0:128], AF.Exp, [Bpb[gb]], [X["BEG"]])
                    act(X["DT"], X["DT"], AF.Exp, [X["BDT"], Bgd], [X["BDT"]], bias=colg(ngcg), scale=1.0)
                    act(X["Dm"], X["Dm"], AF.Exp, [X["BDm"], Bgd], [X["BDm"]], bias=colg(gcg), scale=-1.0)
                    yield
                    tt("pool", X["Dm"], X["Dm"], C(IDENT), ALU.subtract, [X["BDm"], Bcst], [X["BDm"]])
                    stt(X["A"], X["KK"], col(beta), X["Dm"], ALU.mult, ALU.mult, [X["BKK"], Bab, X["BDm"]], [X["BA"]])
                    tt("pool", X["A16"], X["A"], C(BD16), ALU.mult, [X["BA"], Bcst], [X["BA16"]])
                    tt("pool", X["inT"], X["QK"], X["DT"], ALU.mult, [X["BQK"], X["BDT"]], [X["BinT"]])
                    tt("pool", X["qd"], qT[:, ts_], X["EG"], ALU.mult, [Bq, X["BEG"]], [X["Bqd"]])
                    yield
                    bk = sm_bank()
                    tr(bk, pbank[bk][:, 0:128], X["A16"], [X["BA16"]])
                    cp("act", X["Bm"], pbank[bk][:, 0:128], [Bpb[bk]], [X["BBm"]])
                    tt("pool", X["Y"], C(IDENT), X["Bm"], ALU.subtract, [Bcst, X["BBm"]], [X["BY"]])
                    yield
                    b1 = sm_bank()
                    mm1(b1, pbank[b1][:, 0:128], X["A16"], X["Bm"], [X["BA16"], X["BBm"]])
                    S.op("pe", lambda e, b1=b1: e.matmul(pbank[b1][:, 128:256], X["Bm"], X["A16"], start=True, stop=True),
                         reads=[X["BA16"], X["BBm"]], writes=[])
                    Bpb[b1].lw = ("eng", "pe", S.cnt["pe"])
                    cp("act", X["Pb"], pbank[b1][:, 0:128], [Bpb[b1]], [X["BPb"]])
                    cp("act", X["Pa"], pbank[b1][:, 128:256], [Bpb[b1]], [X["BPa"]])
                    yield
                    pa, pb2, pan, pbn = "Pa", "Pb", "Pa2", "Pb2"
                    for s_ in range(1, 4):
                        b2 = sm_bank()
                        if s_ < 3:
                            b2z = sm_bank()
                            mm1(b2z, pbank[b2z][:, 0:128], X[pa], X["Y"], [X["B" + pa], X["BY"]])
                            mm1(b2, pbank[b2][:, 0:128], X[pa], X[pb2], [X["B" + pa], X["B" + pb2]])
                            S.op("pe", lambda e, b2=b2, pa=pa, pb2=pb2: e.matmul(pbank[b2][:, 128:256], X[pb2], X[pa], start=True, stop=True),
                                 reads=[X["B" + pa], X["B" + pb2]], writes=[])
                            Bpb[b2].lw = ("eng", "pe", S.cnt["pe"])
                            tt("dve", X["Y"], X["Y"], pbank[b2z][:, 0:128], ALU.add, [X["BY"], Bpb[b2z]], [X["BY"]])
                            cp("act", X[pbn], pbank[b2][:, 0:128], [Bpb[b2]], [X["B" + pbn]])
                            cp("act", X[pan], pbank[b2][:, 128:256], [Bpb[b2]], [X["B" + pan]])
                            pa, pb2, pan, pbn = pan, pbn, pa, pb2
                        else:
                            mm1(b2, pbank[b2][:, 0:128], X[pa], X["Y"], [X["B" + pa], X["BY"]])
                            tt("dve", X["Y"], X["Y"], pbank[b2][:, 0:128], ALU.add, [X["BY"], Bpb[b2]], [X["BY"]])
                        yield
                    tsc("pool", X["vb"], vtok[:, tl, :], col(beta), ALU.mult, [Bvtok, Bab], [X["Bvb"]], s2=1.0, op1=ALU.mult)
                    tsc("pool", X["kbg"], ktok[:, tl, :], colg(kbsg), ALU.mult, [Bktok, Bgd], [X["Bkbg"]], s2=1.0, op1=ALU.mult)
                    act(X["kdec"], ktok[:, tl, :], AF.Identity, [Bktok, Bgd], [X["Bkdec"]], scale=colg(edlg))
                    for lv in range(3):
                        mi = (ML0 + lv) if d == 0 else (ML0 + 3 + lv)
                        tt("pool", X["Lb"], X["A"], C(mi), ALU.mult, [X["BA"], Bcst], [X["BLb"]])
                        b7 = sm_bank()
                        mm1(b7, pbank[b7][:, 0:128], X["Lb"], X["Y"], [X["BLb"], X["BY"]])
                        b8 = sm_bank()
                        tr(b8, pbank[b8][:, 0:128], X["Y"], [X["BY"]])
                        cp("act", X["Mt"], pbank[b7][:, 0:128], [Bpb[b7]], [X["BMt"]])
                        cp("act", X["V"], pbank[b8][:, 0:128], [Bpb[b8]], [X["BV"]])
                        yield
                        b9 = sm_bank()
                        mm1(b9, pbank[b9][:, 0:128], X["V"], X["Mt"], [X["BV"], X["BMt"]])
                        tt("dve", X["Y"], X["Y"], pbank[b9][:, 0:128], ALU.subtract, [X["BY"], Bpb[b9]], [X["BY"]])
                        yield
                    b3 = sm_bank()
                    mm1(b3, pbank[b3][:, 0:128], X["kbg"], X["Y"], [X["Bkbg"], X["BY"]])
                    act(X["nuT"], pbank[b3][:, 0:128], AF.Identity, [Bpb[b3]], [X["BnuT"]], scale=-1.0)
                    yield
                    Scur = stt_["Scur"]
                    if step > 0 and step % 2 == 0:
                        tsc("dve", Sst[Scur], Sst[Scur], flag[:, 0:1], ALU.mult, [BS[Scur], Bflag], [BS[Scur]])
                    b4 = sm_bank()
                    mmg(b4, pbank[b4][:, 0:128], [(X["Y"], X["vb"], [X["BY"], X["Bvb"]]),
                                                  (X["nuT"], Sst[Scur], [X["BnuT"], BS[Scur]])])
                    cp("act", X["vnew"], pbank[b4][:, 0:128], [Bpb[b4]], [X["Bvnew"]])
                    yield
                    b5 = sm_bank()
                    mmg(b5, pbank[b5][:, 0:128], [(Sst[Scur], X["qd"], [BS[Scur], X["Bqd"]]),
                                                  (X["vnew"], X["inT"], [X["Bvnew"], X["BinT"]])])
                    b6 = sm_bank()
                    mm1(b6, pbank[b6][:, 0:128], X["kdec"], X["vnew"], [X["Bkdec"], X["Bvnew"]])
                    tt("dve", oTf[:, ts_], oTf[:, ts_], pbank[b5][:, 0:128], ALU.add, [BoTf, Bpb[b5]], [BoTf])
                    Snx = Scur ^ 1
                    stt(Sst[Snx], Sst[Scur], colg(eglg), pbank[b6][:, 0:128], ALU.mult, ALU.add, [BS[Scur], Bgd, Bpb[b6]], [BS[Snx]])
                    stt_["Scur"] = Snx
                    if step % 2 == 1:
                        seg = tl // 2
                        S.dma("sp", ns_d[d][seg, h], Sst[Snx], reads=[BS[Snx]], final=True)
                    yield

                if KDN >= 4:
                    if os.environ.get("KSCN") and S.limit is None:
                        S.limit = int(os.environ["KSCN"])
                    S.op("pool", lambda e: e.memset(oTf, 0.0), reads=[], writes=[BoTf])
                    chs = []
                    for d in range(2):
                        S.dma("sp", SS[d][0], s0_d[d][h], writes=[BSS[d][0]])
                        chs.append({"d": d, "order": list(range(NT)) if d == 0 else list(range(NT - 1, -1, -1)),
                                    "next": 0, "active": [], "st": {"Scur": 0}})
                    LAG = 9
                    while any(c_["next"] < NT or c_["active"] for c_ in chs):
                        for c_ in chs:
                            if c_["next"] < NT and len(c_["active"]) < 2 and (not c_["active"] or c_["active"][-1]["n"] >= LAG):
                                step = c_["next"]
                                c_["next"] += 1
                                Xs = XS[c_["d"] * 2 + step % 2]
                                c_["active"].append({"g": tile_gen(c_["d"], Xs, SS[c_["d"]], BSS[c_["d"]], c_["st"], step, c_["order"][step]), "n": 0})
                            for a_ in list(c_["active"]):
                                try:
                                    next(a_["g"])
                                    a_["n"] += 1
                                except StopIteration:
                                    c_["active"].remove(a_)
                if KDN < 6:
                    continue
                bks = [sm_bank(), sm_bank()]
                act(tmpa[0][:], oTf, AF.Square, [BoTf], [Btmpa[0]])
                for h2 in range(2):
                    mm1(bks[h2], pbank[bks[h2]][:, :], C(ONES), tmpa[0][:, h2 * 512:(h2 + 1) * 512], [Btmpa[0], Bcst])
                    act(tmpa[1][:, h2 * 512:(h2 + 1) * 512], pbank[bks[h2]][:, :], AF.Sqrt, [Bpb[bks[h2]]], [Btmpa[1]],
                        bias=epsc[:, 0:1], scale=1.0 / 128.0)
                S.op("dve", lambda e: e.reciprocal(out=tmpa[1][:], in_=tmpa[1][:]), reads=[Btmpa[1]], writes=[Btmpa[1]])
                tt("dve", tmpa[1][:], tmpa[1][:], zg, ALU.mult, [Btmpa[1], Bz], [Btmpa[1]])
                tt("dve", og[:, hh, :], oTf, tmpa[1][:], ALU.mult, [BoTf, Btmpa[1]], [Bog[hh]])

            def evo(ci, c, tb, bk, ps):
                xs = xT[:, c, tb * 512:(tb + 1) * 512]
                stt(xs, ps, mod[0][:, 2 * 16 + c:2 * 16 + c + 1], xs, ALU.mult, ALU.add, [Bpb[bk], Bmod[0], BxT[c]], [BxT[c]])
            if KDN >= 6:
                proj(dnwout_d, 2 * g, 2, list(range(KC)), og, Bog, T, evo)
        S.limit = None
        S.fence()
        print("DN region max cols", rst.get("max"), "of", RW)
    dump()

    rreset()
    if STAGE >= 3:
        uT = ralloc([128, 16, T], BF16)
        BuT = [buf("uT%d" % i) for i in range(16)]
        ffn(0, uT, BuT)
        S.fence()
    dump()

    if STAGE >= 4:
        adaln(1)
        norm_mod(1, nmw[:, 1, :], 0, 1)
    S.fence()
    rreset()
    if STAGE >= 4:
        def rs2(name, shape, dt=F32):
            return ralloc(shape, dt)
        cmbin = rs2("cmbin", [128, 64])
        cmlnw = rs2("cmlnw", [128, 32])
        cmlnb = rs2("cmlnb", [128, 32])
        wsT = rs2("wsT", [128, 16, 128], BF16)
        wsn = [rs2("wsn%d" % i, [128, 128]) for i in range(2)]
        bsr = rs2("bsr", [128, 128])
        uTc = rs2("uTc", [128, 2, T])
        vln = rs2("vln", [128, 2, T])
        vlt = rs2("vlt", [128, NT, 256], BF16)
        mT = rs2("mT", [128, 2, T], BF16)
        mu = rs2("mu", [128, T])
        gtmp = [rs2("gtmp%d" % i, [128, 512]) for i in range(2)]
        gsq = [rs2("gsq%d" % i, [128, 512]) for i in range(2)]
        Bcm, BwsT, Bbsr, BuTc, Bvln, Bvlt, Bmu = buf("cmsmall"), buf("wsT"), buf("bsr"), buf("uTc"), buf("vln"), buf("vlt"), buf("mu")
        Bwsn = [buf("wsn0"), buf("wsn1")]
        BmT = [buf("mT0"), buf("mT1")]
        Bgtmp = [buf("gtmp0"), buf("gtmp1")]
        Bgsq = [buf("gsq0"), buf("gsq1")]
        S.dma("sp", cmbin[:], cmbin_d, writes=[Bcm])
        S.dma("sp", cmlnw[:], cmlnw_d, writes=[Bcm])
        S.dma("sp", cmlnb[:], cmlnb_d, writes=[Bcm])
        S.op("dve", lambda e: e.memset(bsr, 0.0), reads=[], writes=[Bbsr])
        for g in range(16):
            S.dma("sp", wsn[g % 2], cmws_d[g], writes=[Bwsn[g % 2]])
            bk = sm_bank()
            tr(bk, pbank[bk][:, 0:128], wsn[g % 2], [Bwsn[g % 2]])
            cp("act", wsT[:, g, :], pbank[bk][:, 0:128], [Bpb[bk]], [BwsT])
        bsum = [6, 7]
        bsq = [4, 5]
        st["sm"] = 5
        cnt_v = [0]

        def evs(ci, c, tb, bk, ps):
            i2 = cnt_v[0] % 2
            first = cnt_v[0] < 2
            last = cnt_v[0] >= 62
            cnt_v[0] += 1
            act(gtmp[i2], ps, AF.Gelu_apprx_tanh, [Bpb[bk], Bcm], [Bgtmp[i2]], bias=cmbin[:, c:c + 1])
            tt("dve", gsq[i2], gtmp[i2], gtmp[i2], ALU.mult, [Bgtmp[i2]], [Bgsq[i2]])
            S.op("pe", (lambda e, o=pbank[bsum[tb]][:, :], r=gtmp[i2]: e.matmul(o, C(ONES), r, start=first, stop=last)),
                 reads=[Bgtmp[i2], Bcst], writes=[Bpb[bsum[tb]]] if first else [])
            S.op("pe", (lambda e, o=pbank[bsq[tb]][:, :], r=gsq[i2]: e.matmul(o, C(ONES), r, start=first, stop=last)),
                 reads=[Bgsq[i2], Bcst], writes=[Bpb[bsq[tb]]] if first else [])
        proj(cmwin_d, 0, KC, list(range(32, 64)), hT, BhT, T, evs)
        for h2 in range(2):
            Bpb[bsum[h2]].lw = Bpb[bsq[h2]].lw = ("eng", "pe", S.cnt["pe"])
        for h2 in range(2):
            hs = slice(h2 * 512, (h2 + 1) * 512)
            tsc("dve", mu[:, hs], pbank[bsum[h2]][:, :], 1.0 / 4096.0, ALU.mult, [Bpb[bsum[h2]]], [Bmu])
            tt("dve", tmpa[0][:, hs], mu[:, hs], mu[:, hs], ALU.mult, [Bmu], [Btmpa[0]])
            stt(tmpa[0][:, hs], pbank[bsq[h2]][:, :], 1.0 / 4096.0, tmpa[0][:, hs], ALU.mult, ALU.subtract, [Bpb[bsq[h2]], Btmpa[0]], [Btmpa[0]])
            act(rstd[:, hs], tmpa[0][:, hs], AF.Sqrt, [Btmpa[0]], [Brstd], bias=epsc[:, 0:1], scale=1.0)
        S.op("dve", lambda e: e.reciprocal(out=rstd[:], in_=rstd[:]), reads=[Brstd], writes=[Brstd])
        for g in range(16):
            S.dma("sp", bsr[0:1, :], cmbs_d[0:1, g * 128:(g + 1) * 128], writes=[Bbsr])

            def evu(ci, c, tb, bk, ps):
                act(uTc[:, ci, tb * 512:(tb + 1) * 512], ps, AF.Gelu_apprx_tanh, [Bpb[bk], Bcm], [BuTc], bias=cmbin[:, c:c + 1])
            proj(cmwin_d, 0, KC, [2 * g, 2 * g + 1], hT, BhT, T, evu)

            def evv(ci, c, tb, bk, ps):
                act(vln[:, ci, tb * 512:(tb + 1) * 512], ps, AF.Gelu_apprx_tanh, [Bpb[bk], Bcm], [Bvln], bias=cmbin[:, c:c + 1])
            proj(cmwin_d, 0, KC, [32 + 2 * g, 32 + 2 * g + 1], hT, BhT, T, evv)
            for fc in range(2):
                c = 2 * g + fc
                tt("dve", vln[:, fc, :], vln[:, fc, :], mu[:], ALU.subtract, [Bvln, Bmu], [Bvln])
                tt("dve", vln[:, fc, :], vln[:, fc, :], rstd[:], ALU.mult, [Bvln, Brstd], [Bvln])
                tsc("dve", vln[:, fc, :], vln[:, fc, :], cmlnw[:, c:c + 1], ALU.mult, [Bvln, Bcm], [Bvln], s2=cmlnb[:, c:c + 1], op1=ALU.add)
                for t4 in range(2):
                    bk = sm_bank()
                    for j in range(4):
                        tl = t4 * 4 + j
                        S.op("pe", (lambda e, o=pbank[bk][:, j * 128:(j + 1) * 128], i=vln[:, fc, tl * 128:(tl + 1) * 128]:
                                    e.transpose(o, i, C(IDENT))), reads=[Bvln, Bcst], writes=[Bpb[bk]] if j == 0 else [])
                    Bpb[bk].lw = ("eng", "pe", S.cnt["pe"])
                    cp("act", vlt[:, t4 * 4:(t4 + 1) * 4, fc * 128:(fc + 1) * 128],
                       pbank[bk][:, :].rearrange("p (j t) -> p j t", j=4), [Bpb[bk]], [Bvlt])
            for fc in range(2):
                for t4 in range(2):
                    bk = sm_bank()
                    for j in range(4):
                        tl = t4 * 4 + j
                        o = pbank[bk][:, j * 128:(j + 1) * 128]
                        S.op("pe", (lambda e, o=o, tl=tl, fc=fc, g=g: e.matmul(o, vlt[:, tl, fc * 128:(fc + 1) * 128], wsT[:, g, :], start=True, stop=False)),
                             reads=[Bvlt, BwsT], writes=[Bpb[bk]] if j == 0 else [])
                        S.op("pe", (lambda e, o=o: e.matmul(o, C(ONES), bsr, start=False, stop=True)),
                             reads=[Bcst, Bbsr], writes=[])
                    Bpb[bk].lw = ("eng", "pe", S.cnt["pe"])
                    tt("dve", mT[:, fc, t4 * 512:(t4 + 1) * 512], pbank[bk][:, :], uTc[:, fc, t4 * 512:(t4 + 1) * 512], ALU.mult,
                       [Bpb[bk], BuTc], [BmT[fc]])

            def evo2(ci, c, tb, bk, ps):
                xs = xT[:, c, tb * 512:(tb + 1) * 512]
                stt(xs, ps, mod[1][:, 2 * 16 + c:2 * 16 + c + 1], xs, ALU.mult, ALU.add, [Bpb[bk], Bmod[1], BxT[c]], [BxT[c]])
            proj(cmwout_d, 2 * g, 2, list(range(KC)), mT, BmT, T, evo2)
        S.fence()
    dump()

    rreset()
    if STAGE >= 5:
        uT = ralloc([128, 16, T], BF16)
        BuT = [buf("uT1_%d" % i) for i in range(16)]
        ffn(1, uT, BuT)
        S.fence()
    dump()

    rms_stats()
    S.fence()
    rreset()
    yo = [ralloc([128, D]) for i in range(2)]
    Byo = [buf("yo0"), buf("yo1")]
    for c in range(KC):
        stt(xT[:, c, :], xT[:, c, :], fnw[:, c:c + 1], rstd[:], ALU.mult, ALU.mult, [BxT[c], Bsm, Brstd], [BxT[c]])
    for tl in range(NT):
        yb = tl % 2
        for c4 in range(4):
            bk = sm_bank()
            for j in range(4):
                c = c4 * 4 + j
                S.op("pe", (lambda e, o=pbank[bk][:, j * 128:(j + 1) * 128], i=xT[:, c, tl * 128:(tl + 1) * 128]:
                            e.transpose(o, i, C(IDENT))), reads=[BxT[c], Bcst], writes=[Bpb[bk]] if j == 0 else [])
            Bpb[bk].lw = ("eng", "pe", S.cnt["pe"])
            cp("act" if c4 % 2 == 0 else "dve", yo[yb][:, c4 * 512:(c4 + 1) * 512], pbank[bk][:, :], [Bpb[bk]], [Byo[yb]])
        S.dma("sp", y_d[tl * 128:(tl + 1) * 128, :], yo[yb][:], reads=[Byo[yb]], final=True)

    with ExitStack() as stk:
        S.emit(stk)
    return nc


_NC = None


def _fm(v, n):
    return np.ascontiguousarray(np.asarray(v, np.float32).reshape(n, 128).T)


def kernel(x_prompt, x_sample, state_dn_fwd, state_dn_bwd, c, c_ctx, norm_mix_w, norm_mlp_w,
           w_ada, b_ada, dn_w_in, dn_conv_w, dn_A_log, dn_dt_bias, dn_norm_w, dn_w_out,
           cm_w_in, cm_b_in, cm_ln_w, cm_ln_b, cm_w_s, cm_b_s, cm_w_out, w_ff1, w_ff2, final_norm_w):
    global _NC
    if _NC is None:
        _NC = build()
    nc = _NC
    f32 = np.float32
    A = lambda a: np.ascontiguousarray(np.asarray(a, f32))
    r = np.arange(128)
    ident = np.eye(128, dtype=f32)
    ones = np.ones((128, 128), f32)
    Uf = (r[:, None] <= r[None, :]).astype(f32)
    Ub = (r[:, None] >= r[None, :]).astype(f32)
    BIG = 1e30
    M1f = np.where(r[:, None] > r[None, :], -BIG, 0).astype(f32)
    M2f = np.where(r[None, :] > r[:, None], BIG, 0).astype(f32)
    M1b = np.where(r[:, None] < r[None, :], -BIG, 0).astype(f32)
    M2b = np.where(r[None, :] < r[:, None], BIG, 0).astype(f32)
    bd16 = (r[:, None] // 16 == r[None, :] // 16).astype(f32)
    mls = []
    for b_ in (16, 32, 64):
        same = (r[:, None] // (2 * b_) == r[None, :] // (2 * b_))
        mls.append((same & (r[:, None] % (2 * b_) >= b_) & (r[None, :] % (2 * b_) < b_)).astype(f32))
    mlb = [m_.T.copy() for m_ in mls]
    cst = np.stack([ident, ones, Uf, Ub, bd16, M1f, M2f, M1b, M2b] + mls + mlb, axis=1)
    posc = np.zeros((128, 84), f32)
    posc[:, 0:4] = np.arange(4)[None, :] * 128 + r[:, None]
    posc[:, 4:20] = np.arange(16)[None, :]
    posc[:, 20:84] = np.arange(64)[None, :]
    shared = {
        "cst": np.ascontiguousarray(cst), "posc": posc,
        "nmw": np.stack([_fm(norm_mix_w[l], 16) for l in range(2)]),
        "nlw": np.stack([_fm(norm_mlp_w[l], 16) for l in range(2)]),
        "fnw": _fm(final_norm_w, 16),
        "bada": np.stack([_fm(b_ada[l], 96) for l in range(2)]),
        "w_ada": A(w_ada), "dn_w_in": A(dn_w_in[0]),
        "convw": np.ascontiguousarray(np.asarray(dn_conv_w[0], f32).reshape(5, 64, 128).transpose(2, 1, 0)),
        "alog": np.ascontiguousarray(np.broadcast_to(np.asarray(dn_A_log[0], f32).reshape(1, 64), (128, 64))),
        "dtb": np.ascontiguousarray(np.broadcast_to(np.asarray(dn_dt_bias[0], f32).reshape(1, 64), (128, 64))),
        "dnnw": A(dn_norm_w[0]).reshape(128, 1),
        "dn_w_out": A(dn_w_out[0]), "cm_w_in": A(cm_w_in[0]),
        "cmbin": _fm(cm_b_in[0], 64), "cmlnw": _fm(cm_ln_w[0], 32), "cmlnb": _fm(cm_ln_b[0], 32),
        "cm_w_s": A(cm_w_s[0]), "cmbs": A(cm_b_s[0]).reshape(1, 2048),
        "cm_w_out": A(cm_w_out[0]), "w_ff1": A(w_ff1), "w_ff2": A(w_ff2),
    }
    xp = np.asarray(x_prompt, f32)
    xs = np.asarray(x_sample, f32)
    zeros_s = np.zeros((32, 128, 128), f32)
    in_maps = []
    roles = [("ctx", 0), ("ctx", 1), ("ctx", 2), ("ctx", 3), ("lat", 0), ("lat", 1), ("ctx", 0), ("ctx", 1)]
    for kind, i in roles:
        m = {}
        if kind == "ctx":
            m["x"] = np.ascontiguousarray(xp[4 * i:4 * i + 4].reshape(T, D))
            m["flag"] = np.zeros((128, 1), f32)
            m["cond"] = _fm(c_ctx, 16)
            m["s0f"] = zeros_s
            m["s0b"] = zeros_s
        else:
            m["x"] = np.ascontiguousarray(xs[i])
            m["flag"] = np.ones((128, 1), f32)
            m["cond"] = _fm(c[i], 16)
            m["s0f"] = A(state_dn_fwd[i, 0])
            m["s0b"] = A(state_dn_bwd[i, 0])
        m.update(shared)
        in_maps.append({"i_" + k: v for k, v in m.items()})
    sel = os.environ.get("KCORES")
    if sel:
        idx = [int(v) for v in sel.split(",")]
        res = run_bass_kernel_spmd(nc, [in_maps[i] for i in idx], core_ids=list(range(len(idx))))
        R = [None] * 8
        for j, i in enumerate(idx):
            R[i] = res.results[j]
        for i in range(8):
            if R[i] is None:
                R[i] = res.results[0]
    else:
        res = run_bass_kernel_spmd(nc, in_maps, core_ids=list(range(8)))
        R = res.results
    y_prompt = np.concatenate([R[i]["o_y"].reshape(4, 256, D) for i in range(4)], axis=0)
    y_sample = np.stack([R[4]["o_y"], R[5]["o_y"]], axis=0)
    nsf = np.concatenate([R[i]["o_nsf"] for i in range(4)], axis=0)[:, None]
    nsb = np.concatenate([R[i]["o_nsb"] for i in range(4)], axis=0)[:, None]
    if DBG:
        kernel.dbg = [R[i]["o_dbg"] for i in range(8)]
    return (y_prompt.astype(f32), y_sample.astype(f32), nsf.astype(f32), nsb.astype(f32))
```
